# Optimizing a Trainium2 kernel written in Bass

```python
import jax
import jax.numpy as jnp
from jax import lax
import numpy as np


D_MODEL = 1024
BATCH = 2
SEQ = 8192
DEPTH = 1

CHUNK = 64
D_MIX = D_MODEL
RET_WIDTH = D_MIX // 2
RET_HEADS = 4
RET_DK = RET_WIDTH // RET_HEADS
RET_DV = RET_WIDTH // RET_HEADS
ROPE_BASE = 10000.0
GLA_WIDTH = D_MIX - RET_WIDTH
GLA_HEADS = 4
GLA_KEY_WIDTH = GLA_WIDTH // 2
GLA_DK = GLA_KEY_WIDTH // GLA_HEADS
GLA_DV = GLA_WIDTH // GLA_HEADS
GLA_GATE_RANK = 16
GLA_GATE_TAU = 16.0
D_FF = 4 * D_MODEL
LN_EPS = 1e-5
DEEPNORM_ALPHA = (2.0 * DEPTH) ** 0.25
DEEPNORM_BETA = (8.0 * DEPTH) ** -0.25
IN_SPLITS = (RET_WIDTH, RET_WIDTH, RET_WIDTH, RET_WIDTH,
             GLA_KEY_WIDTH, GLA_KEY_WIDTH, GLA_WIDTH, GLA_WIDTH, GLA_GATE_RANK)
D_IN_PROJ = sum(IN_SPLITS)
VALUE_SLOTS = (2, 6)

kernel_name = 'hybrid_retention_gla_deepnorm_adaln'


def _layer_norm(x, w=None, b=None):
    xf = x.astype(jnp.float32)
    mu = jnp.mean(xf, axis=-1, keepdims=True)
    var = jnp.mean(jnp.square(xf - mu), axis=-1, keepdims=True)
    y = (xf - mu) * lax.rsqrt(var + LN_EPS)
    if w is not None:
        y = y * w.astype(jnp.float32) + b.astype(jnp.float32)
    return y.astype(x.dtype)


def _head_norm(o, w, center):
    if center:
        o = o - jnp.mean(o, axis=-1, keepdims=True)
    o = o * lax.rsqrt(jnp.mean(jnp.square(o), axis=-1, keepdims=True) + LN_EPS)
    B, S, H, d = o.shape
    return o.reshape(B, S, H * d) * w.astype(jnp.float32)


def _rotary(x, pos):
    half = x.shape[-1] // 2
    inv = 1.0 / (ROPE_BASE ** jnp.linspace(0.0, 1.0, half, dtype=jnp.float32))
    ang = pos[:, None] * inv[None, :]
    cos = jnp.cos(ang)[:, None, :]
    sin = jnp.sin(ang)[:, None, :]
    x1, x2 = x[..., :half], x[..., half:]
    return jnp.concatenate([x1 * cos - x2 * sin, x2 * cos + x1 * sin], axis=-1)


def _retention(q, k, v):
    B, S, H, dk = q.shape
    dv = v.shape[-1]
    N = S // CHUNK
    log_gamma = jnp.log(1.0 - 2.0 ** (-5.0 - jnp.arange(H, dtype=jnp.float32)))
    pos = jnp.arange(S, dtype=jnp.float32)
    q = _rotary(q, pos) * (dk ** -0.5)
    k = _rotary(k, pos)
    qc = q.reshape(B, N, CHUNK, H, dk)
    kc = k.reshape(B, N, CHUNK, H, dk)
    vc = v.reshape(B, N, CHUNK, H, dv)
    idx = jnp.arange(CHUNK, dtype=jnp.float32)
    dist = jnp.abs(idx[:, None] - idx[None, :])
    intra_decay = jnp.exp(log_gamma[:, None, None] * dist)
    scores = jnp.einsum('bnchd,bnshd->bnhcs', qc, kc) * intra_decay
    o = jnp.einsum('bnhcs,bnshe->bnche', scores, vc)
    k_decay = jnp.exp(log_gamma[None, :] * (CHUNK - 1.0 - idx)[:, None])
    q_decay = jnp.exp(log_gamma[None, :] * (idx + 1.0)[:, None])
    chunk_decay = jnp.exp(log_gamma * CHUNK)[None, :, None, None]
    U = jnp.einsum('bnshd,bnshe->nbhde', kc * k_decay[:, :, None], vc)

    def step(R, U_i):
        return chunk_decay * R + U_i, R

    _, R_prev = lax.scan(step, jnp.zeros((B, H, dk, dv), jnp.float32), U)
    o = o + jnp.einsum('bnchd,nbhde->bnche', qc * q_decay[:, :, None], R_prev)
    return o.reshape(B, S, H, dv)


def _gla(q, k, v, log_a):
    B, S, H, dk = q.shape
    dv = v.shape[-1]
    N = S // CHUNK
    q = q * (dk ** -0.5)

    def to_chunks(t):
        return t.reshape(B, N, CHUNK, H, t.shape[-1]).transpose(1, 0, 2, 3, 4)

    def step(state, inp):
        qi, ki, vi, lai = inp
        b = jnp.cumsum(lai, axis=1)
        decay = jnp.exp(-jnp.abs(b[:, :, None] - b[:, None, :]))
        A = jnp.einsum('bthd,bshd,btshd->bhts', qi, ki, decay)
        o = jnp.einsum('bhts,bshe->bthe', A, vi)
        o = o + jnp.einsum('bthd,bhde->bthe', qi * jnp.exp(b), state)
        b_last = b[:, -1]
        new_state = jnp.exp(b_last)[..., None] * state + jnp.einsum(
            'bshd,bshe->bhde', ki * jnp.exp(b_last[:, None] - b), vi)
        return new_state, o

    _, o = lax.scan(step, jnp.zeros((B, H, dk, dv), jnp.float32),
                    (to_chunks(q), to_chunks(k), to_chunks(v), to_chunks(log_a)))
    return o.transpose(1, 0, 2, 3, 4).reshape(B, S, H, dv)


def _token_mixer(u, w_in, ret_norm_w, gla_gate_w, gla_gate_b, gla_norm_w, w_out):
    B, S, _ = u.shape
    proj = (u @ w_in).astype(jnp.float32)
    cuts = [int(v) for v in np.cumsum(IN_SPLITS)[:-1]]
    rq, rk, rv, rg, gq, gk, gv, gg, glr = jnp.split(proj, cuts, axis=-1)
    ro = _retention(rq.reshape(B, S, RET_HEADS, RET_DK),
                    rk.reshape(B, S, RET_HEADS, RET_DK),
                    rv.reshape(B, S, RET_HEADS, RET_DV))
    ro = _head_norm(ro, ret_norm_w, center=True) * jax.nn.silu(rg)
    gate_logit = glr @ gla_gate_w.astype(jnp.float32) + gla_gate_b.astype(jnp.float32)
    log_a = jax.nn.log_sigmoid(gate_logit) / GLA_GATE_TAU
    go = _gla(gq.reshape(B, S, GLA_HEADS, GLA_DK),
              gk.reshape(B, S, GLA_HEADS, GLA_DK),
              gv.reshape(B, S, GLA_HEADS, GLA_DV),
              log_a.reshape(B, S, GLA_HEADS, GLA_DK))
    go = _head_norm(go, gla_norm_w, center=False) * jax.nn.silu(gg)
    mixed = jnp.concatenate([ro, go], axis=-1).astype(u.dtype)
    return mixed @ w_out


def setup_inputs(seed: int = 0) -> dict:
    key = jax.random.key(seed)
    ks = jax.random.split(key, 16)
    nrm = jax.random.normal
    offs = np.concatenate([[0], np.cumsum(IN_SPLITS)])
    col_scale = np.ones((D_IN_PROJ,), np.float32)
    for slot in VALUE_SLOTS:
        col_scale[int(offs[slot]):int(offs[slot + 1])] = DEEPNORM_BETA
    return {
        'x': nrm(ks[0], (BATCH, SEQ, D_MODEL), jnp.float32),
        'c': nrm(ks[1], (BATCH, D_MODEL), jnp.float32),
        'w_ada': nrm(ks[2], (DEPTH, D_MODEL, 6 * D_MODEL), jnp.float32) * (0.5 * D_MODEL ** -0.5),
        'b_ada': 0.02 * nrm(ks[3], (DEPTH, 6 * D_MODEL), jnp.float32),
        'w_in': nrm(ks[4], (DEPTH, D_MODEL, D_IN_PROJ), jnp.float32) * (D_MODEL ** -0.5) * jnp.asarray(col_scale),
        'ret_norm_w': 1.0 + 0.02 * nrm(ks[5], (DEPTH, RET_WIDTH), jnp.float32),
        'gla_gate_w': nrm(ks[6], (DEPTH, GLA_GATE_RANK, GLA_KEY_WIDTH), jnp.float32) * (GLA_GATE_RANK ** -0.5),
        'gla_gate_b': 0.1 * nrm(ks[7], (DEPTH, GLA_KEY_WIDTH), jnp.float32),
        'gla_norm_w': 1.0 + 0.02 * nrm(ks[8], (DEPTH, GLA_WIDTH), jnp.float32),
        'w_out': nrm(ks[9], (DEPTH, D_MIX, D_MODEL), jnp.float32) * (D_MIX ** -0.5) * DEEPNORM_BETA,
        'ln1_w': 1.0 + 0.02 * nrm(ks[10], (DEPTH, D_MODEL), jnp.float32),
        'ln1_b': 0.02 * nrm(ks[11], (DEPTH, D_MODEL), jnp.float32),
        'w_ff1': nrm(ks[12], (DEPTH, D_MODEL, D_FF), jnp.float32) * (D_MODEL ** -0.5) * DEEPNORM_BETA,
        'w_ff2': nrm(ks[13], (DEPTH, D_FF, D_MODEL), jnp.float32) * (D_FF ** -0.5) * DEEPNORM_BETA,
        'ln2_w': 1.0 + 0.02 * nrm(ks[14], (DEPTH, D_MODEL), jnp.float32),
        'ln2_b': 0.02 * nrm(ks[15], (DEPTH, D_MODEL), jnp.float32),
    }


def reference(x, c, w_ada, b_ada, w_in, ret_norm_w, gla_gate_w, gla_gate_b, gla_norm_w,
              w_out, ln1_w, ln1_b, w_ff1, w_ff2, ln2_w, ln2_b):
    for l in range(DEPTH):
        mod = jax.nn.silu(c) @ w_ada[l] + b_ada[l]
        shift1, scale1, gate1, shift2, scale2, gate2 = jnp.split(mod, 6, axis=-1)
        u = _layer_norm(x) * (1.0 + scale1[:, None, :]) + shift1[:, None, :]
        m = _token_mixer(u, w_in[l], ret_norm_w[l], gla_gate_w[l], gla_gate_b[l],
                         gla_norm_w[l], w_out[l])
        x = _layer_norm(DEEPNORM_ALPHA * x + gate1[:, None, :] * m, ln1_w[l], ln1_b[l])
        u2 = _layer_norm(x) * (1.0 + scale2[:, None, :]) + shift2[:, None, :]
        f = jnp.square(jax.nn.relu(u2 @ w_ff1[l])) @ w_ff2[l]
        x = _layer_norm(DEEPNORM_ALPHA * x + gate2[:, None, :] * f, ln2_w[l], ln2_b[l])
    return x
```

```python
import contextlib
import numpy as np
import concourse.bass as bass
import concourse.mybir as mybir
from concourse.bass_utils import run_bass_kernel_spmd

F32 = mybir.dt.float32
BF16 = mybir.dt.bfloat16
ALU = mybir.AluOpType
AF = mybir.ActivationFunctionType
AX = mybir.AxisListType

NPRE = 16
NB = 16
GRP = 2
ALPHA = float(2.0 ** 0.25)
EPS = 1e-5
GAM = [1.0 - 2.0 ** (-5.0 - h) for h in range(4)]

C_ID, C_MR, C_ML, C_MU, C_TN, C_TR, C_KD, C_ER, C_N16, C_ONE, C_VM = 0, 128, 640, 768, 896, 1024, 1152, 1156, 1160, 1161, 1289
C_ACT = C_VM + 48
C_AR = C_ACT + 8
NCST = C_AR + 32


class T:
    __slots__ = ("name", "w", "r", "excl")

    def __init__(self, name, excl=False):
        self.name = name
        self.w = None
        self.r = {}
        self.excl = excl


class Sched:
    def __init__(self, nc, stack):
        self.nc = nc
        self.stack = stack
        self.E = {"pe": nc.tensor, "act": nc.scalar, "dve": nc.vector,
                  "pool": nc.gpsimd, "sp": nc.sync}
        self.sems = {}
        self.cnt = {}
        self.known = {e: {} for e in self.E}
        self.newsem("cc")
        for e in ("pe", "act", "dve", "pool"):
            self.newsem(e)
        self.nwaits = 0
        self.nins = 0
        self.dead = False

    def newsem(self, key):
        h = self.stack.enter_context(self.nc.semaphore(key.replace(":", "_")))
        self.sems[key] = h
        self.cnt[key] = 0

    def _wait(self, eng, evs):
        need = {}
        for (k, v) in evs:
            if eng == "pe" and k == "pe":
                continue
            if need.get(k, 0) < v:
                need[k] = v
        kn = self.known[eng]
        for k, v in need.items():
            if kn.get(k, 0) < v:
                self.E[eng].wait_ge(self.sems[k], v)
                kn[k] = v
                self.nwaits += 1

    def _deps(self, reads, writes):
        evs = []
        for t in reads:
            if t.w is not None:
                evs.append(t.w)
        for t in writes:
            if t.w is not None:
                evs.append(t.w)
            evs.extend(t.r.items())
        return evs

    def _commit(self, ev, reads, writes):
        for t in reads:
            if t.r.get(ev[0], 0) < ev[1]:
                t.r[ev[0]] = ev[1]
        for t in writes:
            t.w = ev
            t.r = {}

    def op(self, eng, fn, reads=(), writes=()):
        if self.dead:
            return
        if any(t.excl for t in reads):
            writes = list(writes) + [t for t in reads if t.excl]
            reads = [t for t in reads if not t.excl]
        self._wait(eng, self._deps(reads, writes))
        ins = fn(self.E[eng])
        self.cnt[eng] += 1
        ins.then_inc(self.sems[eng], 1)
        self.nins += 1
        self._commit((eng, self.cnt[eng]), reads, writes)

    def dma(self, q, out, in_, reads=(), writes=(), key=None, extra=(), **kw):
        if self.dead:
            return
        if key is None:
            key = "d:" + (writes[0].name if writes else reads[0].name)
        if key not in self.sems:
            self.newsem(key)
        self._wait(q, self._deps(reads, writes) + list(extra))
        self.E[q].dma_start(out=out, in_=in_, **kw).then_inc(self.sems[key], 16)
        self.cnt[key] += 16
        self.nins += 1
        self._commit((key, self.cnt[key]), reads, writes)

    def all_compute_evs(self):
        return [(e, self.cnt[e]) for e in ("pe", "act", "dve", "pool") if self.cnt[e] > 0]

    def finish(self, eng, ts):
        evs = []
        for t in ts:
            if t.w is not None:
                evs.append(t.w)
            evs.extend(t.r.items())
        self._wait(eng, evs)


class StopBuild(Exception):
    pass


def build_nc(debug=False, npre=NPRE, nb=NB, stop=0):
    nc = bass.Bass("TRN2", target_bir_lowering=False)

    def D(name, shape, dt=F32, kind="ExternalInput"):
        return nc.dram_tensor(name, shape, dt, kind=kind).ap()

    xm = D("xm", [NB, 128, 1024])
    rot = D("rot", [NB, 128, 192])
    c8 = D("c8", [128, 8])
    w_ada = D("w_ada", [1024, 6144])
    b_ada = D("b_ada", [1, 6144])
    w_in = D("w_in", [1024, 3600])
    w_out = D("w_out", [1024, 1024])
    w_ff1 = D("w_ff1", [1024, 4096])
    w_ff2 = D("w_ff2", [4096, 1024])
    rows = D("rows", [5, 1024])
    wg = D("wg", [17, 256])
    cst = D("cst", [128, NCST])
    out = D("out", [NB, 128, 1024], kind="ExternalOutput")
    x1d = D("x1d", [NB, 128, 1024], kind=("ExternalOutput" if debug else "Internal"))
    xin = nc.dram_tensor("xin", [128, 1024], F32)
    xg = nc.dram_tensor("xg", [4 * 128, 1024], F32)

    with contextlib.ExitStack() as st:
        S = Sched(nc, st)

        def ck(k):
            if stop == k:
                S.dead = True

        try:
            def sb(name, shape, dt=F32):
                return st.enter_context(nc.sbuf_tensor(name, shape, dt))

            def ps(name, shape, dt=F32):
                return st.enter_context(nc.psum_tensor(name, shape, dt))

            pAB = ps("pAB", [128, 2, 512]); TpA = T("pA", True); TpB = T("pB", True)
            pT = ps("pT", [128, 8, 128], BF16); TpT = T("pT", True)
            pS = ps("pS", [128, 512]); TpS = T("pS", True)
            pO = ps("pO", [128, 512]); TpO = T("pO", True)
            pU = ps("pU", [128, 512]); TpU = T("pU", True)
            pG = ps("pG", [128, 512]); TpG = T("pG", True)
            pP = ps("pP", [128, 512]); TpP = T("pP", True)
            banks = [(pAB[:, 0, :], TpA), (pAB[:, 1, :], TpB)]

            identb = sb("identb", [128, 128], BF16); Tid = T("identb")
            gate_bc = sb("gate_bc", [128, 2, 1024]); Tgate = T("gate_bc")
            modc = sb("modc", [128, 4, 8]); Tmodc = T("modc")
            S.dma("pool", identb[:], cst[:, C_ID:C_ID + 128], writes=[Tid])
            w1_bf = sb("w1_bf", [128, 8, 4096], BF16); Tw1 = T("w1_bf")

            with contextlib.ExitStack() as st1:
                def sb1(name, shape, dt=F32):
                    return st1.enter_context(nc.sbuf_tensor(name, shape, dt))

                cs = sb1("cs", [128, NCST]); Tcs = T("cs")
                wgs = sb1("wgs", [17, 256]); Twg = T("wgs")
                rowbc = sb1("rowbc", [128, 3, 1024]); Trow = T("rowbc")
                R32 = sb1("R32", [128, 4, 128]); TR32 = T("R32")
                Rbf = sb1("Rbf", [128, 4, 128], BF16); TRbf = T("Rbf")
                S32 = sb1("S32", [128, 2, 128]); TS32 = T("S32")
                Sbf = sb1("Sbf", [128, 2, 128], BF16); TSbf = T("Sbf")
                glrT = sb1("glrT", [17, 128]); TglrT = T("glrT")

                ident_f = cs[:, C_ID:C_ID + 128]
                maskR = cs[:, C_MR:C_MR + 512].rearrange("p (h t) -> p h t", h=4)
                ML = cs[:, C_ML:C_ML + 128]
                MU = cs[:, C_MU:C_MU + 128]
                TriN = cs[:, C_TN:C_TN + 128]
                TriR = cs[:, C_TR:C_TR + 128]
                kdec = cs[:, C_KD:C_KD + 4]
                epsr = cs[:, C_ER:C_ER + 4]
                n16 = cs[:, C_N16:C_N16 + 1]
                ones_row = cs[0:1, C_ONE:C_ONE + 128]
                actv = cs[:, C_ACT:C_ACT + 8]
                aRt = cs[:, C_AR:C_AR + 32]

                S.dma("sp", cs[:], cst, writes=[Tcs])
                S.dma("sp", wgs[:], wg, writes=[Twg])
                for ri, r in enumerate((0, 1, 4)):
                    S.dma("sp", rowbc[:, ri, :], rows[r:r + 1, :].to_broadcast([128, 1024]), writes=[Trow])
                S.op("pool", lambda e: e.memset(R32[:], 0.0), writes=[TR32])
                S.op("pool", lambda e: e.memset(S32[:], 0.0), writes=[TS32])
                S.op("pool", lambda e: e.memset(Rbf[:], 0.0), writes=[TRbf])
                S.op("pool", lambda e: e.memset(Sbf[:], 0.0), writes=[TSbf])
                S.op("pool", lambda e: e.memset(glrT[:], 1.0), writes=[TglrT])
                ck(1)

                w_in_bf = w1_bf[:, :, 0:3712]; TwinA = [T("wina%d" % i) for i in range(3)]; TwinB = [T("winb%d" % i) for i in range(3)]
                w_out_bf = sb1("w_out_bf", [128, 8, 1024], BF16); Twout = T("w_out_bf")
                ck(2)

                w_in_v = w_in.rearrange("(k p) n -> p k n", p=128)
                sBb = sb1("sBb", [128, 8, 128], BF16); TsB = T("sBb")
                dtmp2 = sb1("dtmp2", [128, 2, 128]); Tdtmp2 = T("dtmp2")
                c8s = sb1("c8s", [128, 8]); Tc8 = T("c8s")
                S.dma("sp", c8s[:], c8, writes=[Tc8])
                S.op("act", lambda e: e.activation(c8s[:], c8s[:], AF.Silu), reads=[Tc8], writes=[Tc8])
                S.op("dve", lambda e: e.tensor_copy(sBb[:], c8s[:].unsqueeze(2).to_broadcast([128, 8, 128])),
                     reads=[Tc8], writes=[TsB])
                w_ada_v = w_ada.rearrange("(k p) n -> p k n", p=128)

                def ada_sub(i):
                    col0 = i * 256
                    wb, Twb, bb, Tbb = wstp[i % 2][:], Twstp[i % 2], bstp[i % 2][:], Tbstp[i % 2]
                    S.dma("pool", wb, w_ada_v[:, :, col0:col0 + 256], writes=[Twb])
                    S.dma("sp", bb, b_ada[0:1, col0:col0 + 256], writes=[Tbb])
                    bank, Tbank = banks[i % 2]

                    def mm(e):
                        for k in range(8):
                            e.matmul(bank[:, 0:256], lhsT=sBb[:, k, :], rhs=wb[:, k, :], start=(k == 0), stop=False)
                        return e.matmul(bank[:, 0:256], lhsT=ones_row, rhs=bb, start=False, stop=True)
                    S.op("pe", mm, reads=[TsB, Twb, Tbb, Tcs], writes=[Tbank])
                    kind = col0 // 1024
                    off = col0 % 1024
                    if kind in (2, 5):
                        gi = 0 if kind == 2 else 1
                        S.op("act", lambda e: e.activation(gate_bc[:, gi, off:off + 256], bank[:, 0:256], AF.Copy),
                             reads=[Tbank], writes=[Tgate])
                    else:
                        mi = {0: 0, 1: 1, 3: 2, 4: 3}[kind]
                        k0 = off // 128
                        S.op("dve", lambda e: e.tensor_tensor(
                            dtmp2[:], bank[:, 0:256].rearrange("p (a b) -> p a b", a=2),
                            ident_f.unsqueeze(1).to_broadcast([128, 2, 128]), ALU.mult),
                            reads=[Tbank, Tcs], writes=[Tdtmp2])
                        S.op("dve", lambda e: e.tensor_reduce(modc[:, mi, k0:k0 + 2], dtmp2[:], AX.X, ALU.add),
                             reads=[Tdtmp2], writes=[Tmodc])
                        if off == 768 and kind in (1, 4):
                            S.op("dve", lambda e: e.tensor_scalar(modc[:, mi, :], modc[:, mi, :], 1.0, None, ALU.add),
                                 reads=[Tmodc], writes=[Tmodc])

                ada_barrier = []
                ck(3)
                allT = []

                def mk(name, shape, dt=F32, n=1):
                    bufs = [sb1("%s_%d" % (name, i), shape, dt) for i in range(n)]
                    ts = [T("%s_%d" % (name, i)) for i in range(n)]
                    allT.extend(ts)
                    return (bufs, ts) if n > 1 else (bufs[0], ts[0])

                xt, Txt = mk("xt", [128, 1024], F32, 4)
                rt, Trt = mk("rt", [128, 192], F32, 3)
                stA, TstA = mk("stA", [128, 2, 6]); mvA, TmvA = mk("mvA", [128, 2]); rsA, TrsA = mk("rsA", [128, 2])
                xn, Txn = mk("xn", [128, 1024], BF16, 2)
                uT, TuT = mk("uT", [128, 8, 128], BF16, 2)
                TuTb = [T("uTb0"), T("uTb1")]; allT.extend(TuTb)
                qf, Tqf = mk("qf", [128, 512], F32, 2)
                kf, Tkf = mk("kf", [128, 512], F32, 2)
                rvb, Trvb = mk("rvb", [128, 512], BF16, 2)
                gvb, Tgvb = mk("gvb", [128, 512], BF16, 2)
                srg, Tsrg = mk("srg", [128, 512], BF16, 2)
                sgg, Tsgg = mk("sgg", [128, 512], BF16, 2)
                gqkT, TgqkT = mk("gqkT", [128, 4, 128], F32, 2)
                gkt, Tgkt = mk("gkt", [128, 256], F32, 2)
                glrT = [glrT, sb1("glrT1", [17, 128])]; TglrT = [TglrT, T("glrT1")]
                S.op("pool", lambda e: e.memset(glrT[1][:], 1.0), writes=[TglrT[1]])
                t1, Tt1 = mk("t1", [128, 512]); t2, Tt2 = mk("t2", [128, 512])
                t3, Tt3 = mk("t3", [128, 512]); t4, Tt4 = mk("t4", [128, 512])
                qr, Tqr = mk("qr", [128, 512], BF16); kr, Tkr = mk("kr", [128, 512], BF16); kd, Tkd = mk("kd", [128, 512], BF16)
                qkT, TqkT = mk("qkT", [128, 8, 128], BF16)
                scT, TscT = mk("scT", [128, 4, 128], BF16)
                spb, Tspb = mk("spb", [128, 256])
                EpT, TEpT = mk("EpT", [128, 2, 128]); EmT, TEmT = mk("EmT", [128, 2, 128])
                Erev, TErev = mk("Erev", [128, 256]); eBl, TeBl = mk("eBl", [128, 2])
                QpT, TQpT = mk("QpM", [128, 4, 128], BF16); QmT, TQmT = mk("QmM", [128, 4, 128], BF16)
                KmT, TKmT = mk("KmT", [128, 2, 128], BF16); KpT, TKpT = mk("KpT", [128, 2, 128], BF16)
                Kd, TKd = mk("Kd", [128, 256], BF16)
                AT, TAT = mk("AT", [128, 4, 128], BF16)
                st4, Tst4 = mk("st4", [128, 4, 6]); mv4, Tmv4 = mk("mv4", [128, 4, 2]); rs4, Trs4 = mk("rs4", [128, 4])
                ss4, Tss4 = mk("ss4", [128, 4]); rg4, Trg4 = mk("rg4", [128, 4])
                mixed, Tmixed = mk("mixed", [128, 1024], BF16, 2)
                mixT, TmixT = mk("mixT", [128, 8, 128], BF16)
                TmixTb = T("mixTb"); allT.append(TmixTb)
                stB, TstB = mk("stB", [128, 2, 6]); mvB, TmvB = mk("mvB", [128, 2]); rsB, TrsB = mk("rsB", [128, 2])
                x1o, Tx1o = mk("x1o", [128, 1024], F32, 2)
                mhalf, Tmh = mk("mhalf", [128, 4])
                Dacc, TDacc = mk("Dacc", [128, 2])
                ctmp, Tctmp = mk("ctmp", [128, 512])
                aone, Taone = mk("aone", [128, 2])
                Tx1d = [T("x1d%d" % i) for i in range(NB)]
                for tt in allT:
                    for (k_, v_) in ada_barrier:
                        if tt.r.get(k_, 0) < v_:
                            tt.r[k_] = v_
                S.op("pool", lambda e: e.memset(QpT[:], 0.0), writes=[TQpT])
                S.op("pool", lambda e: e.memset(QmT[:], 0.0), writes=[TQmT])
                S.op("pool", lambda e: e.memset(mhalf[:], -0.5), writes=[Tmh])
                wstp = [x1o[i][:].bitcast(BF16).rearrange("p (k n) -> p k n", k=8)[:, :, 0:256] for i in range(2)]
                wstp = [type("V", (), {"__getitem__": lambda self, key, v=v: v})() for v in wstp]
                Twstp = [Tx1o[0], Tx1o[1]]
                bstp = [type("V", (), {"__getitem__": lambda self, key, v=v: v})() for v in (t3[0:1, 0:256], t4[0:1, 0:256])]
                Tbstp = [Tt3, Tt4]
                for i_ in range(8):
                    ada_sub(i_)
                    if i_ == 1:
                        for ci_, (c0_, c1_) in enumerate(((512, 1536), (2304, 3072), (3584, 3600))):
                            S.dma("pool", w_in_bf[:, :, c0_:c1_], w_in_v[:, :, c0_:c1_], writes=[TwinA[ci_]], key="d:wina%d" % ci_)

                for ci_, (c0_, c1_) in enumerate(((0, 512), (1536, 2304), (3072, 3584))):
                    S.dma("pool", w_in_bf[:, :, c0_:c1_], w_in_v[:, :, c0_:c1_], writes=[TwinB[ci_]], key="d:winb%d" % ci_)
                S.dma("pool", w_out_bf[:], w_out.rearrange("(k p) n -> p k n", p=128), writes=[Twout])
                S.op("pool", lambda e: e.memset(Dacc[:], 1.0), writes=[TDacc])

                Txin = T("xin")
                Txg = T("xg")

                def exchange():
                    S.dma("sp", xin[:, 0:512], R32[:].rearrange("p h e -> p (h e)"), reads=[TR32], writes=[Txin], key="d:xin")
                    S.dma("sp", xin[:, 512:768], S32[:].rearrange("p a e -> p (a e)"), reads=[TS32], writes=[Txin], key="d:xin")
                    S.dma("sp", xin[:, 768:770], Dacc[:], reads=[TDacc], writes=[Txin], key="d:xin")
                    S._wait("pool", S._deps([Txin], [Txg]))
                    nc.gpsimd.collective_compute("AllGather", ALU.bypass, replica_groups=[[0, 1, 2, 3], [4, 5, 6, 7]],
                                                 ins=[xin.ap().opt()], outs=[xg.ap().opt()]).then_inc(S.sems["cc"])
                    S.cnt["cc"] += 1
                    S._commit(("cc", S.cnt["cc"]), [Txin], [Txg])

                def combine():
                    S.op("pool", lambda e: e.memset(R32[:], 0.0), writes=[TR32])
                    S.op("pool", lambda e: e.memset(S32[:], 0.0), writes=[TS32])
                    for r in range(4):
                        stg, Tstg = x1o[r % 2], Tx1o[r % 2]
                        S.dma("sp", stg[:, 0:770], xg[r * 128:(r + 1) * 128, 0:770], reads=[Txg], writes=[Tstg], key="d:x1o%d" % (r % 2))
                        act_r = actv[:, r:r + 1]
                        S.op("dve", lambda e, stg=stg, act_r=act_r: e.tensor_scalar(ctmp[:], stg[:, 0:512], act_r, None, ALU.mult),
                             reads=[Tstg, Tcs], writes=[Tctmp])

                        def fr(e, r=r):
                            for h in range(4):
                                ii = e.scalar_tensor_tensor(R32[:, h, :], R32[:, h, :], aRt[:, r * 4 + h:r * 4 + h + 1],
                                                            ctmp[:, h * 128:(h + 1) * 128], ALU.mult, ALU.add)
                            return ii
                        S.op("dve", fr, reads=[Tctmp, TR32, Tcs], writes=[TR32])
                        S.op("dve", lambda e, stg=stg, act_r=act_r: e.tensor_scalar(aone[:], stg[:, 768:770], -1.0, act_r, ALU.add, ALU.mult),
                             reads=[Tstg, Tcs], writes=[Taone])
                        S.op("dve", lambda e: e.tensor_scalar(aone[:], aone[:], 1.0, None, ALU.add), reads=[Taone], writes=[Taone])
                        S.op("dve", lambda e, stg=stg, act_r=act_r: e.tensor_scalar(ctmp[:, 0:256], stg[:, 512:768], act_r, None, ALU.mult),
                             reads=[Tstg, Tcs], writes=[Tctmp])

                        def fs2(e):
                            for pr in range(2):
                                ii = e.scalar_tensor_tensor(S32[:, pr, :], S32[:, pr, :], aone[:, pr:pr + 1],
                                                            ctmp[:, pr * 128:(pr + 1) * 128], ALU.mult, ALU.add)
                            return ii
                        S.op("dve", fs2, reads=[Tctmp, TS32, Taone], writes=[TS32])
                    S.op("act", lambda e: e.activation(Rbf[:], R32[:], AF.Copy), reads=[TR32], writes=[TRbf])
                    S.op("act", lambda e: e.activation(Sbf[:], S32[:], AF.Copy), reads=[TS32], writes=[TSbf])

                def ln_stats(src, Tsrc, stt, Tst, mv, Tmv, rs, Trs):
                    def f(e):
                        e.bn_stats(stt[:, 0, :], src[:, 0:512])
                        return e.bn_stats(stt[:, 1, :], src[:, 512:1024])
                    S.op("dve", f, reads=[Tsrc], writes=[Tst])
                    S.op("dve", lambda e: e.bn_aggr(mv[:], stt[:].rearrange("p a b -> p (a b)")), reads=[Tst], writes=[Tmv])
                    S.op("dve", lambda e: e.tensor_scalar(rs[:, 0:1], mv[:, 1:2], EPS, None, ALU.add), reads=[Tmv], writes=[Trs])
                    S.op("pool", lambda e: e.tensor_tensor(rs[:, 0:1], rs[:, 0:1], mhalf[:, 0:1], ALU.pow), reads=[Trs, Tmh], writes=[Trs])
                    S.op("dve", lambda e: e.scalar_tensor_tensor(rs[:, 1:2], mv[:, 0:1], -1.0, rs[:, 0:1], ALU.mult, ALU.mult),
                         reads=[Tmv, Trs], writes=[Trs])

                def rotary(eng, src, Tsrc, rtab, Trtab, dst_f, Tdstf, tmp, Ttmp):
                    s4 = src[:].rearrange("p (h a d) -> p h a d", h=4, a=2)
                    d4 = dst_f[:].rearrange("p (h a d) -> p h a d", h=4, a=2)
                    m4 = tmp[:].rearrange("p (h a d) -> p h a d", h=4, a=2)
                    cosb = rtab[:, 0:64].unsqueeze(1).unsqueeze(1).to_broadcast([128, 4, 2, 64])
                    nsin = rtab[:, 64:128].unsqueeze(1).to_broadcast([128, 4, 64])
                    psin = rtab[:, 128:192].unsqueeze(1).to_broadcast([128, 4, 64])
                    S.op(eng, lambda e: e.tensor_tensor(d4, s4, cosb, ALU.mult), reads=[Tsrc, Trtab], writes=[Tdstf])

                    def f(e):
                        e.tensor_tensor(m4[:, :, 0, :], s4[:, :, 1, :], nsin, ALU.mult)
                        return e.tensor_tensor(m4[:, :, 1, :], s4[:, :, 0, :], psin, ALU.mult)
                    S.op(eng, f, reads=[Tsrc, Trtab], writes=[Ttmp])
                    S.op(eng, lambda e: e.tensor_tensor(dst_f[:], dst_f[:], tmp[:], ALU.add), reads=[Tdstf, Ttmp], writes=[Tdstf])

                gctr = [0]

                def nextbank():
                    bk = banks[gctr[0] % 2]
                    gctr[0] += 1
                    return bk

                def is_pre(b):
                    return b < npre

                def blk_src(b):
                    if is_pre(b):
                        return xm[b], b
                    return xm[b - npre], b - npre

                def load_block(b):
                    src_ap, ridx = blk_src(b)
                    S.dma("sp", xt[b % 4][:], src_ap, writes=[Txt[b % 4]])
                    S.dma("sp", rt[b % 3][:], rot[ridx], writes=[Trt[b % 3]])

                def s1_stats(b):
                    x_, Tx_ = xt[b % 4], Txt[b % 4]
                    ln_stats(x_, Tx_, stA, TstA, mvA, TmvA, rsA, TrsA)

                def s1_xn(b):
                    x_, Tx_ = xt[b % 4], Txt[b % 4]
                    S.op("act", lambda e: e.activation(xn[b % 2][:], x_[:], AF.Identity, bias=rsA[:, 1:2], scale=rsA[:, 0:1]),
                         reads=[Tx_, TrsA], writes=[Txn[b % 2]])

                def s1_tr(b):
                    def tr(e):
                        for k in range(8):
                            i = e.transpose(pT[:, k, :], xn[b % 2][:, k * 128:(k + 1) * 128], identb[:])
                        return i
                    S.op("pe", tr, reads=[Txn[b % 2], Tid], writes=[TpT])
                    u_ = uT[b % 2]

                    def ev(e, k0, k1):
                        for k in range(k0, k1):
                            i = e.activation(u_[:, k, :], pT[:, k, :], AF.Identity, bias=modc[:, 0, k:k + 1], scale=modc[:, 1, k:k + 1])
                        return i
                    S.op("act", lambda e: ev(e, 0, 4), reads=[TpT, Tmodc], writes=[TuT[b % 2]])
                    S.op("act", lambda e: ev(e, 4, 8), reads=[TpT, Tmodc], writes=[TuTb[b % 2]])

                def proj_tok(b, c0, ncols, evac):
                    def piece():
                        bank, Tbank = nextbank()

                        def f(e, k0, k1):
                            for k in range(k0, k1):
                                i = e.matmul(bank[:, 0:ncols], lhsT=uT[b % 2][:, k, :], rhs=w_in_bf[:, k, c0:c0 + ncols],
                                             start=(k == 0), stop=(k == 7))
                            return i
                        wdeps = TwinA + ([] if is_pre(b) else TwinB)
                        S.op("pe", lambda e: f(e, 0, 4), reads=[TuT[b % 2]] + wdeps, writes=[Tbank])
                        S.op("pe", lambda e: f(e, 4, 8), reads=[TuTb[b % 2]] + wdeps, writes=[Tbank])
                        evac(bank, Tbank)
                    return piece

                def proj_gk(b):
                    def piece():
                        bank, Tb = nextbank()

                        def fgk(e):
                            for k in range(8):
                                e.matmul(bank[:, 0:256], lhsT=uT[b % 2][:, k, :], rhs=w_in_bf[:, k, 2304:2560], start=(k == 0), stop=(k == 7))
                            for k in range(8):
                                ii = e.matmul(bank[0:16, 256:384], lhsT=w_in_bf[:, k, 3584:3600], rhs=uT[b % 2][:, k, :],
                                              start=(k == 0), stop=(k == 7))
                            return ii
                        S.op("pe", fgk, reads=[TuT[b % 2], TuTb[b % 2]] + TwinA + ([] if is_pre(b) else TwinB), writes=[Tb])
                        S.op("act", lambda e: e.activation(gkt[b % 2][:], bank[:, 0:256], AF.Copy), reads=[Tb], writes=[Tgkt[b % 2]])
                        S.op("act", lambda e: e.activation(glrT[b % 2][0:16, :], bank[0:16, 256:384], AF.Copy),
                             reads=[Tb], writes=[TglrT[b % 2]])
                    return piece

                def proj_gqk(b):
                    def piece():
                        bank, Tb = nextbank()

                        def fq(e):
                            for m in range(4):
                                for k in range(8):
                                    ii = e.matmul(bank[:, m * 128:(m + 1) * 128], lhsT=w_in_bf[:, k, 2048 + m * 128: 2048 + (m + 1) * 128],
                                                  rhs=uT[b % 2][:, k, :], start=(k == 0), stop=(k == 7))
                            return ii
                        S.op("pe", fq, reads=[TuT[b % 2], TuTb[b % 2]] + TwinA + ([] if is_pre(b) else TwinB), writes=[Tb])
                        S.op("act", lambda e: e.activation(gqkT[b % 2][:].rearrange("p a t -> p (a t)"), bank, AF.Copy),
                             reads=[Tb], writes=[TgqkT[b % 2]])
                    return piece

                def ev_copy(dst, Tdst):
                    return lambda bank, Tb: S.op("act", lambda e: e.activation(dst[:], bank, AF.Copy), reads=[Tb], writes=[Tdst])

                def ev_silu(dst, Tdst):
                    return lambda bank, Tb: S.op("act", lambda e: e.activation(dst[:], bank, AF.Silu), reads=[Tb], writes=[Tdst])

                def s2_pieces(b):
                    p = b % 2
                    if is_pre(b):
                        return [proj_tok(b, 512, 512, ev_copy(kf[p], Tkf[p])),
                                proj_tok(b, 1024, 512, ev_copy(rvb[p], Trvb[p])),
                                proj_gk(b),
                                proj_tok(b, 2560, 512, ev_copy(gvb[p], Tgvb[p]))]
                    rg_p = proj_tok(b, 1536, 512, ev_silu(srg[p], Tsrg[p]))
                    gg_p = proj_tok(b, 3072, 512, ev_silu(sgg[p], Tsgg[p]))

                    def silus():
                        rg_p()
                        gg_p()
                    return [proj_tok(b, 0, 512, ev_copy(qf[p], Tqf[p])),
                            proj_tok(b, 512, 512, ev_copy(kf[p], Tkf[p])),
                            proj_gk(b),
                            proj_tok(b, 1024, 512, ev_copy(rvb[p], Trvb[p])),
                            proj_gqk(b),
                            proj_tok(b, 2560, 512, ev_copy(gvb[p], Tgvb[p])),
                            silus]

                def g_logit(b):
                    p = b % 2
                    S.op("pe", lambda e: e.matmul(pG[:, 0:256], lhsT=glrT[p][:, :], rhs=wgs[:, :], start=True, stop=True),
                         reads=[TglrT[p], Twg], writes=[TpG])
                    S.op("act", lambda e: e.activation(spb[:], pG[:, 0:256], AF.Exp, scale=-1.0), reads=[TpG], writes=[Tspb])
                    S.op("act", lambda e: e.activation(spb[:], spb[:], AF.Ln, bias=1.0), reads=[Tspb], writes=[Tspb])

                def g_cums(b, pre):
                    p = b % 2
                    if pre:
                        def f(e):
                            e.matmul(pG[:, 256:512], lhsT=TriR, rhs=spb[:], start=True, stop=True)
                            e.matmul(pP[:, 0:1], lhsT=spb[:, 0:128], rhs=n16, start=True, stop=True)
                            return e.matmul(pP[:, 1:2], lhsT=spb[:, 128:256], rhs=n16, start=True, stop=True)
                        S.op("pe", f, reads=[Tspb, Tcs], writes=[TpG, TpP])
                        S.op("act", lambda e: e.activation(Erev[:], pG[:, 256:512], AF.Exp), reads=[TpG], writes=[TErev])
                        S.op("act", lambda e: e.activation(eBl[:], pP[:, 0:2], AF.Exp), reads=[TpP], writes=[TeBl])
                    else:
                        pBT = pP[:, 0:256].rearrange("p (a t) -> p a t", a=2)

                        def f(e):
                            e.matmul(pG[:, 256:512], lhsT=TriR, rhs=spb[:], start=True, stop=True)
                            e.matmul(pBT[:, 0, :], lhsT=spb[:, 0:128], rhs=TriN, start=True, stop=True)
                            return e.matmul(pBT[:, 1, :], lhsT=spb[:, 128:256], rhs=TriN, start=True, stop=True)
                        S.op("pe", f, reads=[Tspb, Tcs], writes=[TpG, TpP])
                        S.op("act", lambda e: e.activation(Erev[:], pG[:, 256:512], AF.Exp), reads=[TpG], writes=[TErev])
                        S.op("act", lambda e: e.activation(EpT[:], pBT, AF.Exp), reads=[TpP], writes=[TEpT])
                        S.op("act", lambda e: e.activation(EmT[:], pBT, AF.Exp, scale=-1.0), reads=[TpP], writes=[TEmT])
                        S.op("act", lambda e: e.activation(eBl[:], EpT[:, :, 127], AF.Copy), reads=[TEpT], writes=[TeBl])
                    S.op("pool", lambda e: e.tensor_tensor(Kd[:], gkt[p][:], Erev[:], ALU.mult), reads=[Tgkt[p], TErev], writes=[TKd])

                def k_rot(b, want_kr):
                    p = b % 2
                    rotary("dve", kf[p], Tkf[p], rt[b % 3], Trt[b % 3], t1, Tt1, t2, Tt2)
                    if want_kr:
                        S.op("dve", lambda e: e.tensor_copy(kr[:], t1[:]), reads=[Tt1], writes=[Tkr])
                    S.op("pool", lambda e: e.tensor_tensor(
                        kd[:].rearrange("p (h d) -> p h d", h=4), t1[:].rearrange("p (h d) -> p h d", h=4),
                        kdec.unsqueeze(2).to_broadcast([128, 4, 128]), ALU.mult), reads=[Tt1, Tcs], writes=[Tkd])

                def st_U(b):
                    p = b % 2

                    def fu(e):
                        for h in range(4):
                            i = e.matmul(pU[:, h * 128:(h + 1) * 128], lhsT=kd[:, h * 128:(h + 1) * 128],
                                         rhs=rvb[p][:, h * 128:(h + 1) * 128], start=True, stop=True)
                        return i
                    S.op("pe", fu, reads=[Tkd, Trvb[p]], writes=[TpU])

                    def fr(e):
                        for h in range(4):
                            i = e.scalar_tensor_tensor(R32[:, h, :], R32[:, h, :], float(GAM[h] ** 128),
                                                       pU[:, h * 128:(h + 1) * 128], ALU.mult, ALU.add)
                        return i
                    S.op("dve", fr, reads=[TpU, TR32], writes=[TR32])

                def st_Un(b):
                    p = b % 2
                    pV, TpV = (pS, TpS) if is_pre(b) else (pP, TpP)

                    def fg(e):
                        for pr in range(2):
                            i = e.matmul(pV[:, pr * 256:(pr + 1) * 256], lhsT=Kd[:, pr * 128:(pr + 1) * 128],
                                         rhs=gvb[p][:, pr * 256:(pr + 1) * 256], start=True, stop=True)
                        return i
                    S.op("pe", fg, reads=[TKd, Tgvb[p]], writes=[TpV])

                    def fs(e):
                        for pr in range(2):
                            for hh in range(2):
                                rsl = slice(hh * 64, (hh + 1) * 64)
                                i = e.scalar_tensor_tensor(
                                    S32[rsl, pr, :], S32[rsl, pr, :], eBl[rsl, pr:pr + 1],
                                    pV[rsl, pr * 256 + hh * 128: pr * 256 + (hh + 1) * 128], ALU.mult, ALU.add)
                        return i
                    S.op("dve", fs, reads=[TpV, TS32, TeBl], writes=[TS32])
                    if is_pre(b):
                        S.op("dve", lambda e: e.tensor_tensor(Dacc[:], Dacc[:], eBl[:], ALU.mult), reads=[TeBl, TDacc], writes=[TDacc])

                def s3_pre(b):
                    return {"R": [lambda: k_rot(b, False), lambda: st_U(b)],
                            "G": [lambda: g_logit(b), lambda: g_cums(b, True), lambda: st_Un(b)],
                            "B": [], "tailS": []}

                def s3_main(b):
                    p = b % 2
                    n = b - npre
                    o3 = pO[:].rearrange("p (h e) -> p h e", h=4)
                    g3 = pG[:].rearrange("p (h e) -> p h e", h=4)
                    t33 = t3[:].rearrange("p (h e) -> p h e", h=4)
                    t43 = t4[:].rearrange("p (h e) -> p h e", h=4)
                    mx = mixed[p]
                    Tmx = Tmixed[p]

                    def rotq():
                        rotary("pool", qf[p], Tqf[p], rt[b % 3], Trt[b % 3], t3, Tt3, t4, Tt4)
                        S.op("pool", lambda e: e.tensor_copy(qr[:], t3[:]), reads=[Tt3], writes=[Tqr])

                    def rotk():
                        rotary("dve", kf[p], Tkf[p], rt[b % 3], Trt[b % 3], t1, Tt1, t2, Tt2)
                        S.op("dve", lambda e: e.tensor_copy(kr[:], t1[:]), reads=[Tt1], writes=[Tkr])
                        S.op("pool", lambda e: e.tensor_tensor(
                            kd[:].rearrange("p (h d) -> p h d", h=4), t1[:].rearrange("p (h d) -> p h d", h=4),
                            kdec.unsqueeze(2).to_broadcast([128, 4, 128]), ALU.mult), reads=[Tt1, Tcs], writes=[Tkd])

                    def P2():
                        def trqk(e):
                            for h in range(4):
                                e.transpose(pT[:, h, :], qr[:, h * 128:(h + 1) * 128], identb[:])
                            for h in range(4):
                                ii = e.transpose(pT[:, 4 + h, :], kr[:, h * 128:(h + 1) * 128], identb[:])
                            return ii
                        S.op("pe", trqk, reads=[Tqr, Tkr, Tid], writes=[TpT])
                        S.op("act", lambda e: e.activation(qkT[:], pT[:], AF.Copy), reads=[TpT], writes=[TqkT])

                    def P3a():
                        g_cums(b, False)

                    def P3b():
                        Qp4 = QpT[:].rearrange("p (a b) t -> p a b t", b=2)
                        Qm4 = QmT[:].rearrange("p (a b) t -> p a b t", b=2)
                        gq = gqkT[p]

                        def fqp(e):
                            for hh in range(2):
                                rsl = slice(hh * 64, (hh + 1) * 64)
                                ii = e.scalar_tensor_tensor(Qp4[rsl, :, hh, :], gq[rsl, 0:2, :], 0.125, EpT[rsl, :, :], ALU.mult, ALU.mult)
                            return ii
                        S.op("dve", fqp, reads=[TgqkT[p], TEpT], writes=[TQpT])
                        S.op("pool", lambda e: e.tensor_tensor(KmT[:], gq[:, 2:4, :], EmT[:], ALU.mult), reads=[TgqkT[p], TEmT], writes=[TKmT])

                        def fqm(e):
                            for hh in range(2):
                                rsl = slice(hh * 64, (hh + 1) * 64)
                                ii = e.scalar_tensor_tensor(Qm4[rsl, :, hh, :], gq[rsl, 0:2, :], 0.125, EmT[rsl, :, :], ALU.mult, ALU.mult)
                            return ii
                        S.op("dve", fqm, reads=[TgqkT[p], TEmT], writes=[TQmT])
                        S.op("pool", lambda e: e.tensor_tensor(KpT[:], gq[:, 2:4, :], EpT[:], ALU.mult), reads=[TgqkT[p], TEpT], writes=[TKpT])

                    def P4():
                        def fsc(e):
                            for h in range(4):
                                ii = e.matmul(pS[:, h * 128:(h + 1) * 128], lhsT=qkT[:, 4 + h, :], rhs=qkT[:, h, :], start=True, stop=True)
                            return ii
                        S.op("pe", fsc, reads=[TqkT], writes=[TpS])
                        S.op("dve", lambda e: e.tensor_tensor(scT[:], pS[:].rearrange("p (h t) -> p h t", h=4), maskR, ALU.mult),
                             reads=[TpS, Tcs], writes=[TscT])

                    def P6a():
                        def fo(e):
                            for h in range(4):
                                e.matmul(pO[:, h * 128:(h + 1) * 128], lhsT=scT[:, h, :], rhs=rvb[p][:, h * 128:(h + 1) * 128], start=True, stop=False)
                                ii = e.matmul(pO[:, h * 128:(h + 1) * 128], lhsT=qkT[:, h, :], rhs=Rbf[:, h, :], start=False, stop=True)
                            return ii
                        S.op("pe", fo, reads=[TscT, Trvb[p], TqkT, TRbf], writes=[TpO])

                        def fbs(e):
                            for h in range(4):
                                ii = e.bn_stats(st4[:, h, :], pO[:, h * 128:(h + 1) * 128])
                            return ii
                        S.op("dve", fbs, reads=[TpO], writes=[Tst4])

                        def fba(e):
                            for h in range(4):
                                ii = e.bn_aggr(mv4[:, h, :], st4[:, h, :])
                            return ii
                        S.op("dve", fba, reads=[Tst4], writes=[Tmv4])
                        S.op("dve", lambda e: e.tensor_tensor(rs4[:], mv4[:, :, 1], epsr, ALU.add), reads=[Tmv4, Tcs], writes=[Trs4])

                    def P6b():
                        S.op("pool", lambda e: e.tensor_tensor(rs4[:], rs4[:], mhalf[:], ALU.pow), reads=[Trs4, Tmh], writes=[Trs4])
                        S.op("dve", lambda e: e.tensor_tensor(t33, o3, mv4[:, :, 0:1].to_broadcast([128, 4, 128]), ALU.subtract),
                             reads=[TpO, Tmv4], writes=[Tt3])

                    def P6c():
                        S.op("pool", lambda e: e.tensor_tensor(t33, t33, rs4[:].unsqueeze(2).to_broadcast([128, 4, 128]), ALU.mult),
                             reads=[Tt3, Trs4], writes=[Tt3])
                        S.op("pool", lambda e: e.tensor_tensor(t3[:], t3[:], rowbc[:, 2, 0:512], ALU.mult), reads=[Tt3, Trow], writes=[Tt3])

                    def P6d():
                        S.op("dve", lambda e: e.tensor_tensor(mx[:, 0:512], t3[:], srg[p][:], ALU.mult), reads=[Tt3, Tsrg[p]], writes=[Tmx])

                    def P5a():
                        def fp1(e):
                            for h in range(4):
                                ii = e.matmul(pS[:, h * 128:(h + 1) * 128], lhsT=KmT[:, h // 2, :], rhs=QpT[:, h, :], start=True, stop=True)
                            return ii
                        S.op("pe", fp1, reads=[TKmT, TQpT], writes=[TpS])

                        def fp2(e):
                            for h in range(4):
                                ii = e.matmul(pP[:, h * 128:(h + 1) * 128], lhsT=KpT[:, h // 2, :], rhs=QmT[:, h, :], start=True, stop=True)
                            return ii
                        S.op("pe", fp2, reads=[TKpT, TQmT], writes=[TpP])
                        t13 = t1[:].rearrange("p (h t) -> p h t", h=4)
                        t23 = t2[:].rearrange("p (h t) -> p h t", h=4)
                        S.op("dve", lambda e: e.tensor_tensor(t13, pS[:].rearrange("p (h t) -> p h t", h=4),
                                                              ML.unsqueeze(1).to_broadcast([128, 4, 128]), ALU.mult),
                             reads=[TpS, Tcs], writes=[Tt1])
                        S.op("dve", lambda e: e.tensor_tensor(t23, pP[:].rearrange("p (h t) -> p h t", h=4),
                                                              MU.unsqueeze(1).to_broadcast([128, 4, 128]), ALU.mult),
                             reads=[TpP, Tcs], writes=[Tt2])

                    def P5b():
                        S.op("pool", lambda e: e.tensor_tensor(AT[:].rearrange("p h t -> p (h t)"), t1[:], t2[:], ALU.add),
                             reads=[Tt1, Tt2], writes=[TAT])

                    def P7a():
                        def fog(e):
                            for h in range(4):
                                e.matmul(pG[:, h * 128:(h + 1) * 128], lhsT=AT[:, h, :], rhs=gvb[p][:, h * 128:(h + 1) * 128], start=True, stop=False)
                                ii = e.matmul(pG[:, h * 128:(h + 1) * 128], lhsT=QpT[:, h, :], rhs=Sbf[:, h // 2, :], start=False, stop=True)
                            return ii
                        S.op("pe", fog, reads=[TAT, Tgvb[p], TQpT, TSbf], writes=[TpG])
                        S.op("act", lambda e: e.activation(t4[:], pG[:], AF.Square), reads=[TpG], writes=[Tt4])
                        S.op("dve", lambda e: e.tensor_reduce(ss4[:], t43, AX.X, ALU.add), reads=[Tt4], writes=[Tss4])
                        S.op("dve", lambda e: e.tensor_scalar(rg4[:], ss4[:], 1.0 / 128.0, EPS, ALU.mult, ALU.add), reads=[Tss4], writes=[Trg4])

                    def P7b():
                        S.op("pool", lambda e: e.tensor_tensor(rg4[:], rg4[:], mhalf[:], ALU.pow), reads=[Trg4, Tmh], writes=[Trg4])
                        S.op("dve", lambda e: e.tensor_tensor(t43, g3, rg4[:].unsqueeze(2).to_broadcast([128, 4, 128]), ALU.mult),
                             reads=[TpG, Trg4], writes=[Tt4])

                    def P7c():
                        S.op("pool", lambda e: e.tensor_tensor(t4[:], t4[:], rowbc[:, 2, 512:1024], ALU.mult), reads=[Tt4, Trow], writes=[Tt4])
                        S.op("pool", lambda e: e.tensor_tensor(mx[:, 512:1024], t4[:], sgg[p][:], ALU.mult), reads=[Tt4, Tsgg[p]], writes=[Tmx])

                    def state():
                        st_U(b)
                        st_Un(b)
                        S.op("act", lambda e: e.activation(Rbf[:], R32[:], AF.Copy), reads=[TR32], writes=[TRbf])
                        S.op("act", lambda e: e.activation(Sbf[:], S32[:], AF.Copy), reads=[TS32], writes=[TSbf])

                    return {"R": [rotq, rotk, P2, P4, P6a, P6b, P6c, P6d],
                            "G": [lambda: g_logit(b), P3a, P3b, P5a, P5b, P7a, P7b, P7c],
                            "B": [], "tailS": [state]}

                def s3b_main(b):
                    p = b % 2
                    n = b - npre
                    mx, Tmx = mixed[p], Tmixed[p]
                    x_, Tx_ = xt[b % 4], Txt[b % 4]
                    xo, Txo = x1o[n % 2], Tx1o[n % 2]

                    def P8():
                        def trm(e):
                            for k in range(8):
                                ii = e.transpose(pT[:, k, :], mx[:, k * 128:(k + 1) * 128], identb[:])
                            return ii
                        S.op("pe", trm, reads=[Tmx, Tid], writes=[TpT])
                        S.op("act", lambda e: e.activation(mixT[:, 0:4, :], pT[:, 0:4, :], AF.Copy), reads=[TpT], writes=[TmixT])
                        S.op("act", lambda e: e.activation(mixT[:, 4:8, :], pT[:, 4:8, :], AF.Copy), reads=[TpT], writes=[TmixTb])

                    def P9a():
                        for c in range(2):
                            bank, Tb = nextbank()

                            def fm(e, k0, k1, bank=bank, c=c):
                                for k in range(k0, k1):
                                    ii = e.matmul(bank, lhsT=mixT[:, k, :], rhs=w_out_bf[:, k, c * 512:(c + 1) * 512],
                                                  start=(k == 0), stop=(k == 7))
                                return ii
                            S.op("pe", lambda e, fm=fm: fm(e, 0, 4), reads=[TmixT, Twout], writes=[Tb])
                            S.op("pe", lambda e, fm=fm: fm(e, 4, 8), reads=[TmixTb, Twout], writes=[Tb])
                            S.op("dve", lambda e, bank=bank, c=c: e.tensor_tensor(
                                xo[:, c * 512:(c + 1) * 512], bank, gate_bc[:, 0, c * 512:(c + 1) * 512], ALU.mult),
                                reads=[Tb, Tgate], writes=[Txo])
                        S.op("dve", lambda e: e.scalar_tensor_tensor(xo[:], x_[:], ALPHA, xo[:], ALU.mult, ALU.add),
                             reads=[Tx_, Txo], writes=[Txo])

                    def P9b():
                        ln_stats(xo, Txo, stB, TstB, mvB, TmvB, rsB, TrsB)

                    def P9c():
                        S.op("act", lambda e: e.activation(xo[:], xo[:], AF.Identity, bias=rsB[:, 1:2], scale=rsB[:, 0:1]),
                             reads=[Txo, TrsB], writes=[Txo])
                        S.op("dve", lambda e: e.tensor_tensor(xo[:], xo[:], rowbc[:, 0, :], ALU.mult), reads=[Txo, Trow], writes=[Txo])

                    def P9d():
                        S.op("pool", lambda e: e.tensor_tensor(xo[:], xo[:], rowbc[:, 1, :], ALU.add), reads=[Txo, Trow], writes=[Txo])
                        S.dma("sp", x1d[n], xo[:], reads=[Txo], writes=[Tx1d[n]], key="d:x1o%d" % (n % 2))
                    return [P8, P9a, P9b, P9c, P9d]

                nblk = npre + nb
                for b0 in range(min(3, nblk)):
                    load_block(b0)
                pending = list(range(8, 24))
                if npre < 12:
                    for i_ in pending:
                        ada_sub(i_)
                    pending = []
                for t in range(-2, nblk + 1):
                    chains = []
                    tails = []
                    if 0 <= t < nblk:
                        d = s3_pre(t) if is_pre(t) else s3_main(t)
                        if t == npre and npre > 0:
                            exchange()
                            d["R"].insert(4, combine)
                        chains += [d["R"], d["G"]]
                        tails += d["tailS"]
                    if 0 <= t - 1 < nblk and not is_pre(t - 1):
                        chains.append(s3b_main(t - 1))
                    if 0 <= t + 1 < nblk:
                        chains.append(s2_pieces(t + 1))
                    seq = []
                    for i in range(max([len(c_) for c_ in chains] + [0])):
                        for c_ in chains:
                            if i < len(c_):
                                seq.append(c_[i])
                    if 0 <= t + 2 < nblk:
                        st_p = lambda bb=t + 2: s1_stats(bb)
                        xn_p = lambda bb=t + 2: s1_xn(bb)
                        tr_p = lambda bb=t + 2: s1_tr(bb)
                        if 0 <= t < nblk and is_pre(t):
                            k_rot_, logit_, rk_, stU_, cums_, rv_, stUn_ = seq[0:7]
                            seq = [k_rot_, st_p, logit_, rk_, cums_, xn_p, rv_, stU_, stUn_, tr_p] + seq[7:]
                        elif 0 <= t < nblk:
                            seq.insert(min(5, len(seq)), st_p)
                            seq.insert(min(11, len(seq)), xn_p)
                            seq.append(tr_p)
                        else:
                            seq = [st_p, xn_p] + seq + [tr_p]
                    seq += tails
                    for fn_ in seq:
                        fn_()
                    if 3 <= t + 3 < nblk:
                        load_block(t + 3)
                    if pending and t >= 0:
                        ada_sub(pending.pop(0))
                    if t + 1 == nblk - 1 and nb > 0:
                        S.dma("pool", w1_bf[:], w_ff1.rearrange("(k p) n -> p k n", p=128), writes=[Tw1] + TwinA + TwinB)
                    if t == 0:
                        ck(4)
                    if t == npre:
                        ck(9)
                barrier = S.all_compute_evs() + [(k, S.cnt[k]) for k in ("d:x1o0", "d:x1o1") if k in S.cnt]
            with contextlib.ExitStack() as st2:
                def sb2(name, shape, dt=F32):
                    return st2.enter_context(nc.sbuf_tensor(name, shape, dt))
                rowbc2 = sb2("rowbc2", [128, 2, 1024]); Trow2 = T("rowbc2")
                w2_bf = sb2("w2_bf", [128, 32, 1024], BF16); Tw2 = T("w2_bf")
                for ri, r in enumerate((2, 3)):
                    S.dma("sp", rowbc2[:, ri, :], rows[r:r + 1, :].to_broadcast([128, 1024]), writes=[Trow2], extra=barrier)
                S.dma("pool", w2_bf[:], w_ff2.rearrange("(k p) n -> p k n", p=128), writes=[Tw2], extra=barrier)
                ck(10)
                NG = nb // GRP
                x1g = [sb2("x1g%d" % i, [128, GRP, 1024]) for i in range(2)]
                Tx1g = [[T("x1g%d_%d" % (i, j)) for j in range(GRP)] for i in range(2)]
                u2T = [sb2("u2T%d" % i, [128, 8, GRP * 128], BF16) for i in range(2)]; Tu2T = [T("u2T%d" % i) for i in range(2)]
                hT = sb2("hT", [128, 32, GRP * 128], BF16); ThT = [T("hT%d" % i) for i in range(16)]
                rl = [sb2("rl%d" % i, [128, 2 * GRP * 128]) for i in range(2)]; Trl = [T("rl%d" % i) for i in range(2)]
                xn2 = [sb2("xn2_%d" % i, [128, 1024], BF16) for i in range(GRP)]
                stC = sb2("stC", [128, 2, 6]); TstC = T("stC")
                mvC = sb2("mvC", [128, 2]); TmvC = T("mvC")
                rsC = sb2("rsC", [128, 2]); TrsC = T("rsC")
                stD = sb2("stD", [128, 2, 6]); TstD = T("stD")
                mvD = sb2("mvD", [128, 2]); TmvD = T("mvD")
                rsD = sb2("rsD", [128, 2]); TrsD = T("rsD")
                x2p = sb2("x2p", [128, 1024]); Tx2p = T("x2p")
                ob = [sb2("ob%d" % i, [128, 1024]) for i in range(2)]; Tob = [T("ob%d" % i) for i in range(2)]
                Txn2 = [T("xn2_%d" % i) for i in range(GRP)]
                first_done = set()

                def guard(Tt):
                    if Tt.name not in first_done:
                        first_done.add(Tt.name)
                        for (k, v) in barrier:
                            if Tt.r.get(k, 0) < v:
                                Tt.r[k] = v
                for tt in (ThT + [TstC, TmvC, TrsC, TstD, TmvD, TrsD, Tx2p] + Txn2 + Tu2T + Trl + Tob + Tx1g[0] + Tx1g[1]):
                    guard(tt)
                fb = [(pS, TpS), (pO, TpO), (pU, TpU), (pG, TpG)]

                stC2 = [sb2("stC2_%d" % i, [128, 2, 6]) for i in range(GRP)]; TstC2 = [T("stC2_%d" % i) for i in range(GRP)]
                mvC2 = [sb2("mvC2_%d" % i, [128, 2]) for i in range(GRP)]; TmvC2 = [T("mvC2_%d" % i) for i in range(GRP)]
                rsC2 = [sb2("rsC2_%d" % i, [128, 2]) for i in range(GRP)]; TrsC2 = [T("rsC2_%d" % i) for i in range(GRP)]
                for tt in TstC2 + TmvC2 + TrsC2:
                    guard(tt)

                def ln2_a1(g, j):
                    gp = g % 2
                    blk = g * GRP + j
                    src = x1g[gp][:, j, :]
                    Tsrc = Tx1g[gp][j]
                    S.dma("sp", src, x1d[blk], reads=[Tx1d[blk]], writes=[Tsrc])

                    def f(e):
                        e.bn_stats(stC2[j][:, 0, :], src[:, 0:512])
                        return e.bn_stats(stC2[j][:, 1, :], src[:, 512:1024])
                    S.op("dve", f, reads=[Tsrc], writes=[TstC2[j]])
                    S.op("dve", lambda e: e.bn_aggr(mvC2[j][:], stC2[j][:].rearrange("p a b -> p (a b)")), reads=[TstC2[j]], writes=[TmvC2[j]])

                def ln2_a2(g, j):
                    gp = g % 2
                    src = x1g[gp][:, j, :]
                    Tsrc = Tx1g[gp][j]
                    rs_, Trs_, mv_, Tmv_ = rsC2[j], TrsC2[j], mvC2[j], TmvC2[j]
                    S.op("act", lambda e: e.activation(rs_[:, 0:1], mv_[:, 1:2], AF.Ln, bias=EPS), reads=[Tmv_], writes=[Trs_])
                    S.op("act", lambda e: e.activation(rs_[:, 0:1], rs_[:, 0:1], AF.Exp, scale=-0.5), reads=[Trs_], writes=[Trs_])
                    S.op("dve", lambda e: e.scalar_tensor_tensor(rs_[:, 1:2], mv_[:, 0:1], -1.0, rs_[:, 0:1], ALU.mult, ALU.mult),
                         reads=[Tmv_, Trs_], writes=[Trs_])
                    S.op("act", lambda e: e.activation(xn2[j][:], src, AF.Identity, bias=rs_[:, 1:2], scale=rs_[:, 0:1]),
                         reads=[Tsrc, Trs_], writes=[Txn2[j]])

                def ln2_a(g, j):
                    ln2_a1(g, j)
                    ln2_a2(g, j)

                def ln2_b(g, j):
                    gp = g % 2

                    def tr(e):
                        for k in range(8):
                            i = e.transpose(pT[:, k, :], xn2[j][:, k * 128:(k + 1) * 128], identb[:])
                        return i
                    S.op("pe", tr, reads=[Txn2[j], Tid], writes=[TpT])

                    def ev(e):
                        for k in range(8):
                            i = e.activation(u2T[gp][:, k, j * 128:(j + 1) * 128], pT[:, k, :], AF.Identity,
                                             bias=modc[:, 2, k:k + 1], scale=modc[:, 3, k:k + 1])
                        return i
                    S.op("act", ev, reads=[TpT, Tmodc], writes=[Tu2T[gp]])

                def ln2uT(g):
                    for j in range(GRP):
                        ln2_a(g, j)
                        ln2_b(g, j)

                if NG > 0:
                    ln2uT(0)
                    ck(11)
                for g in range(NG):
                    gp = g % 2
                    hT2 = hT[:].rearrange("p (a b) t -> p a (b t)", b=2)
                    for jj in range(16):
                        fbank, Tfb = fb[jj % 4]

                        def f1(e, fbank=fbank, jj=jj):
                            for b_ in range(2):
                                j = 2 * jj + b_
                                for k in range(8):
                                    ii = e.matmul(fbank[:, b_ * 256:(b_ + 1) * 256], lhsT=w1_bf[:, k, j * 128:(j + 1) * 128],
                                                  rhs=u2T[gp][:, k, :], start=(k == 0), stop=(k == 7))
                            return ii
                        S.op("pe", f1, reads=[Tw1, Tu2T[gp]], writes=[Tfb])
                        rr, Trr = rl[jj % 2], Trl[jj % 2]
                        S.op("act", lambda e, fbank=fbank, rr=rr: e.activation(rr[:], fbank[:], AF.Relu), reads=[Tfb], writes=[Trr])
                        S.op("pool", lambda e, rr=rr, jj=jj: e.tensor_tensor(hT2[:, jj, :], rr[:], rr[:], ALU.mult), reads=[Trr], writes=[ThT[jj]])
                        if g + 1 < NG:
                            if jj == 0:
                                ln2_a1(g + 1, 0)
                                ln2_a1(g + 1, 1)
                            elif jj == 5:
                                ln2_a2(g + 1, 0)
                            elif jj == 7:
                                ln2_a2(g + 1, 1)
                            elif jj == 9:
                                ln2_b(g + 1, 0)
                            elif jj == 13:
                                ln2_b(g + 1, 1)
                    ck(12)
                    for j in range(GRP):
                        blk = g * GRP + j
                        for c in range(2):
                            bank, Tb = banks[c]

                            def f2(e, k0, k1, bank=bank, c=c, j=j):
                                for k in range(k0, k1):
                                    ii = e.matmul(bank, lhsT=hT[:, k, j * 128:(j + 1) * 128], rhs=w2_bf[:, k, c * 512:(c + 1) * 512],
                                                  start=(k == 0), stop=(k == 31))
                                return ii
                            S.op("pe", lambda e, f2=f2: f2(e, 0, 24), reads=ThT[0:12] + [Tw2], writes=[Tb])
                            S.op("pe", lambda e, f2=f2: f2(e, 24, 32), reads=ThT[12:16] + [Tw2], writes=[Tb])
                            S.op("dve", lambda e, bank=bank, c=c: e.tensor_tensor(
                                x2p[:, c * 512:(c + 1) * 512], bank, gate_bc[:, 1, c * 512:(c + 1) * 512], ALU.mult),
                                reads=[Tb, Tgate], writes=[Tx2p])
                        S.op("dve", lambda e, j=j: e.scalar_tensor_tensor(x2p[:], x1g[gp][:, j, :], ALPHA, x2p[:], ALU.mult, ALU.add),
                             reads=[Tx1g[gp][j], Tx2p], writes=[Tx2p])

                        def f(e):
                            e.bn_stats(stD[:, 0, :], x2p[:, 0:512])
                            return e.bn_stats(stD[:, 1, :], x2p[:, 512:1024])
                        S.op("dve", f, reads=[Tx2p], writes=[TstD])
                        S.op("dve", lambda e: e.bn_aggr(mvD[:], stD[:].rearrange("p a b -> p (a b)")), reads=[TstD], writes=[TmvD])
                        S.op("act", lambda e: e.activation(rsD[:, 0:1], mvD[:, 1:2], AF.Ln, bias=EPS), reads=[TmvD], writes=[TrsD])
                        S.op("act", lambda e: e.activation(rsD[:, 0:1], rsD[:, 0:1], AF.Exp, scale=-0.5), reads=[TrsD], writes=[TrsD])
                        S.op("dve", lambda e: e.scalar_tensor_tensor(rsD[:, 1:2], mvD[:, 0:1], -1.0, rsD[:, 0:1], ALU.mult, ALU.mult),
                             reads=[TmvD, TrsD], writes=[TrsD])
                        oo, Too = ob[blk % 2], Tob[blk % 2]
                        S.op("act", lambda e, oo=oo: e.activation(oo[:], x2p[:], AF.Identity, bias=rsD[:, 1:2], scale=rsD[:, 0:1]),
                             reads=[Tx2p, TrsD], writes=[Too])
                        S.op("pool", lambda e, oo=oo: e.tensor_tensor(oo[:], oo[:], rowbc2[:, 0, :], ALU.mult), reads=[Too, Trow2], writes=[Too])
                        S.op("pool", lambda e, oo=oo: e.tensor_tensor(oo[:], oo[:], rowbc2[:, 1, :], ALU.add), reads=[Too, Trow2], writes=[Too])
                        S.dma("sp", out[blk], oo[:], reads=[Too], key="d:ob%d" % (blk % 2))
                S.finish("sp", Tob + Tx1o + Tx1d)
                S._wait("sp", [(k, S.cnt[k]) for k in ("d:ob0", "d:ob1", "d:x1o0", "d:x1o1") if k in S.cnt])
        except StopBuild:
            pass
        S.dead = False
        S._wait("sp", S.all_compute_evs() + [(k, v) for k, v in S.cnt.items() if k.startswith("d:")])
        print("kernel build: instructions", S.nins, "waits", S.nwaits)
    return nc


def host_constants(q):
    cst = np.zeros((128, NCST), np.float32)
    cst[:, C_ID:C_ID + 128] = np.eye(128, dtype=np.float32)
    s = np.arange(128, dtype=np.float64)[:, None]
    t = np.arange(128, dtype=np.float64)[None, :]
    allowed = (np.floor(s / 64) <= np.floor(t / 64))
    for h in range(4):
        g = GAM[h]
        m = np.where(allowed, g ** np.abs(t - s) / g ** (t + 1.0), 0.0)
        cst[:, C_MR + h * 128:C_MR + (h + 1) * 128] = m.astype(np.float32)
        cst[:, C_KD + h] = (g ** (127.0 - s[:, 0])).astype(np.float32)
        fac2 = (128.0 ** -1.0) * g ** (2.0 * (s[:, 0] + 1.0))
        cst[:, C_ER + h] = (EPS / fac2).astype(np.float32)
    cst[:, C_ML:C_ML + 128] = (s <= t).astype(np.float32)
    cst[:, C_MU:C_MU + 128] = ((s > t) & (np.floor(s / 64) == np.floor(t / 64))).astype(np.float32)
    cst[:, C_TN:C_TN + 128] = np.where(s <= t, -1.0 / 16.0, 0.0).astype(np.float32)
    cst[:, C_TR:C_TR + 128] = np.where(s > t, -1.0 / 16.0, 0.0).astype(np.float32)
    cst[:, C_N16] = -1.0 / 16.0
    cst[:, C_ONE:C_ONE + 128] = 1.0
    for r in range(4):
        active = (r < q)
        cst[:, C_ACT + r] = 1.0 if active else 0.0
        for h in range(4):
            cst[:, C_AR + r * 4 + h] = np.float32(GAM[h] ** (NB * 128)) if active else 1.0
    return cst


def host_rot(q):
    pos = (q * NB * 128 + np.arange(NB * 128)).astype(np.float32)
    inv = (1.0 / (np.float32(10000.0) ** np.linspace(0.0, 1.0, 64, dtype=np.float32))).astype(np.float32)
    ang = (pos[:, None] * inv[None, :]).astype(np.float32)
    co = np.cos(ang).astype(np.float32)
    si = np.sin(ang).astype(np.float32)
    r = np.concatenate([co, -si, si], axis=1).astype(np.float32)
    return np.ascontiguousarray(r.reshape(NB, 128, 192))


_NC_CACHE = {}


def make_in_maps(inputs):
    x = np.asarray(inputs["x"], np.float32)
    c = np.asarray(inputs["c"], np.float32)
    rows = np.ascontiguousarray(np.stack([
        inputs["ln1_w"][0], inputs["ln1_b"][0], inputs["ln2_w"][0], inputs["ln2_b"][0],
        np.concatenate([inputs["ret_norm_w"][0], inputs["gla_norm_w"][0]])]).astype(np.float32))
    wg = np.ascontiguousarray(np.concatenate([inputs["gla_gate_w"][0], inputs["gla_gate_b"][0][None, :]], 0).astype(np.float32))
    shared = {
        "w_ada": np.ascontiguousarray(inputs["w_ada"][0], dtype=np.float32),
        "b_ada": np.ascontiguousarray(inputs["b_ada"][0][None, :], dtype=np.float32),
        "w_in": np.ascontiguousarray(inputs["w_in"][0], dtype=np.float32),
        "w_out": np.ascontiguousarray(inputs["w_out"][0], dtype=np.float32),
        "w_ff1": np.ascontiguousarray(inputs["w_ff1"][0], dtype=np.float32),
        "w_ff2": np.ascontiguousarray(inputs["w_ff2"][0], dtype=np.float32),
        "rows": rows, "wg": wg,
    }
    in_maps = []
    for j in range(8):
        b, q = j // 4, j % 4
        m = dict(shared)
        m["xm"] = np.ascontiguousarray(x[b, q * 2048:(q + 1) * 2048].reshape(NB, 128, 1024))
        m["c8"] = np.ascontiguousarray(c[b].reshape(8, 128).T)
        m["rot"] = host_rot(q)
        m["cst"] = host_constants(q)
        in_maps.append(m)
    return in_maps


def kernel(**inputs):
    if "nc" not in _NC_CACHE:
        _NC_CACHE["nc"] = build_nc()
    nc = _NC_CACHE["nc"]
    in_maps = make_in_maps(inputs)
    res = run_bass_kernel_spmd(nc, in_maps, core_ids=list(range(8)))
    outp = np.zeros((2, 8192, 1024), np.float32)
    for j in range(8):
        b, q = j // 4, j % 4
        outp[b, q * 2048:(q + 1) * 2048] = np.asarray(res.results[j]["out"]).reshape(2048, 1024)
    return outp
```

```python
import contextlib
import numpy as np
import concourse.bass as bass
import concourse.mybir as mybir
from concourse.bass_utils import run_bass_kernel_spmd

F32 = mybir.dt.float32
BF16 = mybir.dt.bfloat16
ALU = mybir.AluOpType
AF = mybir.ActivationFunctionType
AX = mybir.AxisListType

NPRE = 16
NB = 16
GRP = 2
ALPHA = float(2.0 ** 0.25)
EPS = 1e-5
GAM = [1.0 - 2.0 ** (-5.0 - h) for h in range(4)]

C_ID, C_MR, C_ML, C_MU, C_TN, C_TR, C_KD, C_ER, C_N16, C_ONE, C_VM = 0, 128, 640, 768, 896, 1024, 1152, 1156, 1160, 1161, 1289
C_ACT = C_VM + 48
C_AR = C_ACT + 8
NCST = C_AR + 32


class T:
    __slots__ = ("name", "w", "r", "excl")

    def __init__(self, name, excl=False):
        self.name = name
        self.w = None
        self.r = {}
        self.excl = excl


class Sched:
    def __init__(self, nc, stack):
        self.nc = nc
        self.stack = stack
        self.E = {"pe": nc.tensor, "act": nc.scalar, "dve": nc.vector,
                  "pool": nc.gpsimd, "sp": nc.sync}
        self.sems = {}
        self.cnt = {}
        self.known = {e: {} for e in self.E}
        self.newsem("cc")
        for e in ("pe", "act", "dve", "pool"):
            self.newsem(e)
        self.nwaits = 0
        self.nins = 0
        self.dead = False

    def newsem(self, key):
        h = self.stack.enter_context(self.nc.semaphore(key.replace(":", "_")))
        self.sems[key] = h
        self.cnt[key] = 0

    def _wait(self, eng, evs):
        need = {}
        for (k, v) in evs:
            if eng == "pe" and k == "pe":
                continue
            if need.get(k, 0) < v:
                need[k] = v
        kn = self.known[eng]
        for k, v in need.items():
            if kn.get(k, 0) < v:
                self.E[eng].wait_ge(self.sems[k], v)
                kn[k] = v
                self.nwaits += 1

    def _deps(self, reads, writes):
        evs = []
        for t in reads:
            if t.w is not None:
                evs.append(t.w)
        for t in writes:
            if t.w is not None:
                evs.append(t.w)
            evs.extend(t.r.items())
        return evs

    def _commit(self, ev, reads, writes):
        for t in reads:
            if t.r.get(ev[0], 0) < ev[1]:
                t.r[ev[0]] = ev[1]
        for t in writes:
            t.w = ev
            t.r = {}

    def op(self, eng, fn, reads=(), writes=()):
        if self.dead:
            return
        if any(t.excl for t in reads):
            writes = list(writes) + [t for t in reads if t.excl]
            reads = [t for t in reads if not t.excl]
        self._wait(eng, self._deps(reads, writes))
        ins = fn(self.E[eng])
        self.cnt[eng] += 1
        ins.then_inc(self.sems[eng], 1)
        self.nins += 1
        self._commit((eng, self.cnt[eng]), reads, writes)

    def dma(self, q, out, in_, reads=(), writes=(), key=None, extra=(), **kw):
        if self.dead:
            return
        if key is None:
            key = "d:" + (writes[0].name if writes else reads[0].name)
        if key not in self.sems:
            self.newsem(key)
        self._wait(q, self._deps(reads, writes) + list(extra))
        self.E[q].dma_start(out=out, in_=in_, **kw).then_inc(self.sems[key], 16)
        self.cnt[key] += 16
        self.nins += 1
        self._commit((key, self.cnt[key]), reads, writes)

    def all_compute_evs(self):
        return [(e, self.cnt[e]) for e in ("pe", "act", "dve", "pool") if self.cnt[e] > 0]

    def finish(self, eng, ts):
        evs = []
        for t in ts:
            if t.w is not None:
                evs.append(t.w)
            evs.extend(t.r.items())
        self._wait(eng, evs)


class StopBuild(Exception):
    pass


def build_nc(debug=False, npre=NPRE, nb=NB, stop=0):
    nc = bass.Bass("TRN2", target_bir_lowering=False)

    def D(name, shape, dt=F32, kind="ExternalInput"):
        return nc.dram_tensor(name, shape, dt, kind=kind).ap()

    xm = D("xm", [NB, 128, 1024])
    rot = D("rot", [NB, 128, 192])
    c8 = D("c8", [128, 8])
    w_ada = D("w_ada", [1024, 6144])
    b_ada = D("b_ada", [1, 6144])
    w_in = D("w_in", [1024, 3600])
    w_out = D("w_out", [1024, 1024])
    w_ff1 = D("w_ff1", [1024, 4096])
    w_ff2 = D("w_ff2", [4096, 1024])
    rows = D("rows", [5, 1024])
    wg = D("wg", [17, 256])
    cst = D("cst", [128, NCST])
    out = D("out", [NB, 128, 1024], kind="ExternalOutput")
    x1d = D("x1d", [NB, 128, 1024], kind=("ExternalOutput" if debug else "Internal"))
    xin = nc.dram_tensor("xin", [128, 1024], F32)
    xg = nc.dram_tensor("xg", [4 * 128, 1024], F32)

    with contextlib.ExitStack() as st:
        S = Sched(nc, st)

        def ck(k):
            if stop == k:
                S.dead = True

        try:
            def sb(name, shape, dt=F32):
                return st.enter_context(nc.sbuf_tensor(name, shape, dt))

            def ps(name, shape, dt=F32):
                return st.enter_context(nc.psum_tensor(name, shape, dt))

            pAB = ps("pAB", [128, 2, 512]); TpA = T("pA", True); TpB = T("pB", True)
            pT = ps("pT", [128, 8, 128], BF16); TpT = T("pT", True)
            pS = ps("pS", [128, 512]); TpS = T("pS", True)
            pO = ps("pO", [128, 512]); TpO = T("pO", True)
            pU = ps("pU", [128, 512]); TpU = T("pU", True)
            pG = ps("pG", [128, 512]); TpG = T("pG", True)
            pP = ps("pP", [128, 512]); TpP = T("pP", True)
            banks = [(pAB[:, 0, :], TpA), (pAB[:, 1, :], TpB)]

            identb = sb("identb", [128, 128], BF16); Tid = T("identb")
            gate_bc = sb("gate_bc", [128, 2, 1024]); Tgate = T("gate_bc")
            modc = sb("modc", [128, 4, 8]); Tmodc = T("modc")
            S.dma("pool", identb[:], cst[:, C_ID:C_ID + 128], writes=[Tid])
            w1_bf = sb("w1_bf", [128, 8, 4096], BF16); Tw1 = T("w1_bf")

            with contextlib.ExitStack() as st1:
                def sb1(name, shape, dt=F32):
                    return st1.enter_context(nc.sbuf_tensor(name, shape, dt))

                cs = sb1("cs", [128, NCST]); Tcs = T("cs")
                wgs = sb1("wgs", [17, 256]); Twg = T("wgs")
                rowbc = sb1("rowbc", [128, 3, 1024]); Trow = T("rowbc")
                R32 = sb1("R32", [128, 4, 128]); TR32 = T("R32")
                Rbf = sb1("Rbf", [128, 4, 128], BF16); TRbf = T("Rbf")
                S32 = sb1("S32", [128, 2, 128]); TS32 = T("S32")
                Sbf = sb1("Sbf", [128, 2, 128], BF16); TSbf = T("Sbf")
                glrT = sb1("glrT", [17, 128]); TglrT = T("glrT")

                ident_f = cs[:, C_ID:C_ID + 128]
                maskR = cs[:, C_MR:C_MR + 512].rearrange("p (h t) -> p h t", h=4)
                ML = cs[:, C_ML:C_ML + 128]
                MU = cs[:, C_MU:C_MU + 128]
                TriN = cs[:, C_TN:C_TN + 128]
                TriR = cs[:, C_TR:C_TR + 128]
                kdec = cs[:, C_KD:C_KD + 4]
                epsr = cs[:, C_ER:C_ER + 4]
                n16 = cs[:, C_N16:C_N16 + 1]
                ones_row = cs[0:1, C_ONE:C_ONE + 128]
                actv = cs[:, C_ACT:C_ACT + 8]
                aRt = cs[:, C_AR:C_AR + 32]

                S.dma("sp", cs[:], cst, writes=[Tcs])
                S.dma("sp", wgs[:], wg, writes=[Twg])
                for ri, r in enumerate((0, 1, 4)):
                    S.dma("sp", rowbc[:, ri, :], rows[r:r + 1, :].to_broadcast([128, 1024]), writes=[Trow])
                S.op("pool", lambda e: e.memset(R32[:], 0.0), writes=[TR32])
                S.op("pool", lambda e: e.memset(S32[:], 0.0), writes=[TS32])
                S.op("pool", lambda e: e.memset(Rbf[:], 0.0), writes=[TRbf])
                S.op("pool", lambda e: e.memset(Sbf[:], 0.0), writes=[TSbf])
                S.op("pool", lambda e: e.memset(glrT[:], 1.0), writes=[TglrT])
                ck(1)

                w_in_bf = w1_bf[:, :, 0:3712]; TwinA = [T("wina%d" % i) for i in range(3)]; TwinB = [T("winb%d" % i) for i in range(3)]
                w_out_bf = sb1("w_out_bf", [128, 8, 1024], BF16); Twout = T("w_out_bf")
                ck(2)

                w_in_v = w_in.rearrange("(k p) n -> p k n", p=128)
                sBb = sb1("sBb", [128, 8, 128], BF16); TsB = T("sBb")
                dtmp2 = sb1("dtmp2", [128, 2, 128]); Tdtmp2 = T("dtmp2")
                c8s = sb1("c8s", [128, 8]); Tc8 = T("c8s")
                S.dma("sp", c8s[:], c8, writes=[Tc8])
                S.op("act", lambda e: e.activation(c8s[:], c8s[:], AF.Silu), reads=[Tc8], writes=[Tc8])
                S.op("dve", lambda e: e.tensor_copy(sBb[:], c8s[:].unsqueeze(2).to_broadcast([128, 8, 128])),
                     reads=[Tc8], writes=[TsB])
                w_ada_v = w_ada.rearrange("(k p) n -> p k n", p=128)

                def ada_sub(i):
                    col0 = i * 256
                    wb, Twb, bb, Tbb = wstp[i % 2][:], Twstp[i % 2], bstp[i % 2][:], Tbstp[i % 2]
                    S.dma("pool", wb, w_ada_v[:, :, col0:col0 + 256], writes=[Twb])
                    S.dma("sp", bb, b_ada[0:1, col0:col0 + 256], writes=[Tbb])
                    bank, Tbank = banks[i % 2]

                    def mm(e):
                        for k in range(8):
                            e.matmul(bank[:, 0:256], lhsT=sBb[:, k, :], rhs=wb[:, k, :], start=(k == 0), stop=False)
                        return e.matmul(bank[:, 0:256], lhsT=ones_row, rhs=bb, start=False, stop=True)
                    S.op("pe", mm, reads=[TsB, Twb, Tbb, Tcs], writes=[Tbank])
                    kind = col0 // 1024
                    off = col0 % 1024
                    if kind in (2, 5):
                        gi = 0 if kind == 2 else 1
                        S.op("act", lambda e: e.activation(gate_bc[:, gi, off:off + 256], bank[:, 0:256], AF.Copy),
                             reads=[Tbank], writes=[Tgate])
                    else:
                        mi = {0: 0, 1: 1, 3: 2, 4: 3}[kind]
                        k0 = off // 128
                        S.op("dve", lambda e: e.tensor_tensor(
                            dtmp2[:], bank[:, 0:256].rearrange("p (a b) -> p a b", a=2),
                            ident_f.unsqueeze(1).to_broadcast([128, 2, 128]), ALU.mult),
                            reads=[Tbank, Tcs], writes=[Tdtmp2])
                        S.op("dve", lambda e: e.tensor_reduce(modc[:, mi, k0:k0 + 2], dtmp2[:], AX.X, ALU.add),
                             reads=[Tdtmp2], writes=[Tmodc])
                        if off == 768 and kind in (1, 4):
                            S.op("dve", lambda e: e.tensor_scalar(modc[:, mi, :], modc[:, mi, :], 1.0, None, ALU.add),
                                 reads=[Tmodc], writes=[Tmodc])

                ada_barrier = []
                ck(3)
                allT = []

                def mk(name, shape, dt=F32, n=1):
                    bufs = [sb1("%s_%d" % (name, i), shape, dt) for i in range(n)]
                    ts = [T("%s_%d" % (name, i)) for i in range(n)]
                    allT.extend(ts)
                    return (bufs, ts) if n > 1 else (bufs[0], ts[0])

                xt, Txt = mk("xt", [128, 1024], F32, 4)
                rt, Trt = mk("rt", [128, 192], F32, 3)
                stA, TstA = mk("stA", [128, 2, 6]); mvA, TmvA = mk("mvA", [128, 2]); rsA, TrsA = mk("rsA", [128, 2])
                xn, Txn = mk("xn", [128, 1024], BF16, 2)
                uT, TuT = mk("uT", [128, 8, 128], BF16, 2)
                qf, Tqf = mk("qf", [128, 512], F32, 2)
                kf, Tkf = mk("kf", [128, 512], F32, 2)
                rvb, Trvb = mk("rvb", [128, 512], BF16, 2)
                gvb, Tgvb = mk("gvb", [128, 512], BF16, 2)
                srg, Tsrg = mk("srg", [128, 512], BF16, 2)
                sgg, Tsgg = mk("sgg", [128, 512], BF16, 2)
                gqkT, TgqkT = mk("gqkT", [128, 4, 128], F32, 2)
                gkt, Tgkt = mk("gkt", [128, 256], F32, 2)
                glrT = [glrT, sb1("glrT1", [17, 128])]; TglrT = [TglrT, T("glrT1")]
                S.op("pool", lambda e: e.memset(glrT[1][:], 1.0), writes=[TglrT[1]])
                t1, Tt1 = mk("t1", [128, 512]); t2, Tt2 = mk("t2", [128, 512])
                t3, Tt3 = mk("t3", [128, 512]); t4, Tt4 = mk("t4", [128, 512])
                qr, Tqr = mk("qr", [128, 512], BF16); kr, Tkr = mk("kr", [128, 512], BF16); kd, Tkd = mk("kd", [128, 512], BF16)
                qkT, TqkT = mk("qkT", [128, 8, 128], BF16)
                scT, TscT = mk("scT", [128, 4, 128], BF16)
                spb, Tspb = mk("spb", [128, 256])
                EpT, TEpT = mk("EpT", [128, 2, 128]); EmT, TEmT = mk("EmT", [128, 2, 128])
                Erev, TErev = mk("Erev", [128, 256]); eBl, TeBl = mk("eBl", [128, 2])
                QpT, TQpT = mk("QpM", [128, 4, 128], BF16); QmT, TQmT = mk("QmM", [128, 4, 128], BF16)
                KmT, TKmT = mk("KmT", [128, 2, 128], BF16); KpT, TKpT = mk("KpT", [128, 2, 128], BF16)
                Kd, TKd = mk("Kd", [128, 256], BF16)
                AT, TAT = mk("AT", [128, 4, 128], BF16)
                st4, Tst4 = mk("st4", [128, 4, 6]); mv4, Tmv4 = mk("mv4", [128, 4, 2]); rs4, Trs4 = mk("rs4", [128, 4])
                ss4, Tss4 = mk("ss4", [128, 4]); rg4, Trg4 = mk("rg4", [128, 4])
                mixed, Tmixed = mk("mixed", [128, 1024], BF16, 2)
                mixT, TmixT = mk("mixT", [128, 8, 128], BF16)
                TmixTb = T("mixTb"); allT.append(TmixTb)
                stB, TstB = mk("stB", [128, 2, 6]); mvB, TmvB = mk("mvB", [128, 2]); rsB, TrsB = mk("rsB", [128, 2])
                x1o, Tx1o = mk("x1o", [128, 1024], F32, 2)
                mhalf, Tmh = mk("mhalf", [128, 4])
                Dacc, TDacc = mk("Dacc", [128, 2])
                ctmp, Tctmp = mk("ctmp", [128, 512])
                aone, Taone = mk("aone", [128, 2])
                Tx1d = [T("x1d%d" % i) for i in range(NB)]
                for tt in allT:
                    for (k_, v_) in ada_barrier:
                        if tt.r.get(k_, 0) < v_:
                            tt.r[k_] = v_
                S.op("pool", lambda e: e.memset(QpT[:], 0.0), writes=[TQpT])
                S.op("pool", lambda e: e.memset(QmT[:], 0.0), writes=[TQmT])
                S.op("pool", lambda e: e.memset(mhalf[:], -0.5), writes=[Tmh])
                wstp = [x1o[i][:].bitcast(BF16).rearrange("p (k n) -> p k n", k=8)[:, :, 0:256] for i in range(2)]
                wstp = [type("V", (), {"__getitem__": lambda self, key, v=v: v})() for v in wstp]
                Twstp = [Tx1o[0], Tx1o[1]]
                bstp = [type("V", (), {"__getitem__": lambda self, key, v=v: v})() for v in (t3[0:1, 0:256], t4[0:1, 0:256])]
                Tbstp = [Tt3, Tt4]
                for i_ in range(8):
                    ada_sub(i_)
                    if i_ == 1:
                        for ci_, (c0_, c1_) in enumerate(((512, 1536), (2304, 3072), (3584, 3600))):
                            S.dma("pool", w_in_bf[:, :, c0_:c1_], w_in_v[:, :, c0_:c1_], writes=[TwinA[ci_]], key="d:wina%d" % ci_)

                for ci_, (c0_, c1_) in enumerate(((0, 512), (1536, 2304), (3072, 3584))):
                    S.dma("pool", w_in_bf[:, :, c0_:c1_], w_in_v[:, :, c0_:c1_], writes=[TwinB[ci_]], key="d:winb%d" % ci_)
                S.dma("pool", w_out_bf[:], w_out.rearrange("(k p) n -> p k n", p=128), writes=[Twout])
                S.op("pool", lambda e: e.memset(Dacc[:], 1.0), writes=[TDacc])

                Txin = T("xin")
                Txg = T("xg")

                def exchange():
                    S.dma("sp", xin[:, 0:512], R32[:].rearrange("p h e -> p (h e)"), reads=[TR32], writes=[Txin], key="d:xin")
                    S.dma("sp", xin[:, 512:768], S32[:].rearrange("p a e -> p (a e)"), reads=[TS32], writes=[Txin], key="d:xin")
                    S.dma("sp", xin[:, 768:770], Dacc[:], reads=[TDacc], writes=[Txin], key="d:xin")
                    S._wait("pool", S._deps([Txin], [Txg]))
                    nc.gpsimd.collective_compute("AllGather", ALU.bypass, replica_groups=[[0, 1, 2, 3], [4, 5, 6, 7]],
                                                 ins=[xin.ap().opt()], outs=[xg.ap().opt()]).then_inc(S.sems["cc"])
                    S.cnt["cc"] += 1
                    S._commit(("cc", S.cnt["cc"]), [Txin], [Txg])

                def combine():
                    S.op("pool", lambda e: e.memset(R32[:], 0.0), writes=[TR32])
                    S.op("pool", lambda e: e.memset(S32[:], 0.0), writes=[TS32])
                    for r in range(4):
                        stg, Tstg = x1o[r % 2], Tx1o[r % 2]
                        S.dma("sp", stg[:, 0:770], xg[r * 128:(r + 1) * 128, 0:770], reads=[Txg], writes=[Tstg], key="d:x1o%d" % (r % 2))
                        act_r = actv[:, r:r + 1]
                        S.op("dve", lambda e, stg=stg, act_r=act_r: e.tensor_scalar(ctmp[:], stg[:, 0:512], act_r, None, ALU.mult),
                             reads=[Tstg, Tcs], writes=[Tctmp])

                        def fr(e, r=r):
                            for h in range(4):
                                ii = e.scalar_tensor_tensor(R32[:, h, :], R32[:, h, :], aRt[:, r * 4 + h:r * 4 + h + 1],
                                                            ctmp[:, h * 128:(h + 1) * 128], ALU.mult, ALU.add)
                            return ii
                        S.op("dve", fr, reads=[Tctmp, TR32, Tcs], writes=[TR32])
                        S.op("dve", lambda e, stg=stg, act_r=act_r: e.tensor_scalar(aone[:], stg[:, 768:770], -1.0, act_r, ALU.add, ALU.mult),
                             reads=[Tstg, Tcs], writes=[Taone])
                        S.op("dve", lambda e: e.tensor_scalar(aone[:], aone[:], 1.0, None, ALU.add), reads=[Taone], writes=[Taone])
                        S.op("dve", lambda e, stg=stg, act_r=act_r: e.tensor_scalar(ctmp[:, 0:256], stg[:, 512:768], act_r, None, ALU.mult),
                             reads=[Tstg, Tcs], writes=[Tctmp])

                        def fs2(e):
                            for pr in range(2):
                                ii = e.scalar_tensor_tensor(S32[:, pr, :], S32[:, pr, :], aone[:, pr:pr + 1],
                                                            ctmp[:, pr * 128:(pr + 1) * 128], ALU.mult, ALU.add)
                            return ii
                        S.op("dve", fs2, reads=[Tctmp, TS32, Taone], writes=[TS32])
                    S.op("act", lambda e: e.activation(Rbf[:], R32[:], AF.Copy), reads=[TR32], writes=[TRbf])
                    S.op("act", lambda e: e.activation(Sbf[:], S32[:], AF.Copy), reads=[TS32], writes=[TSbf])

                def ln_stats(src, Tsrc, stt, Tst, mv, Tmv, rs, Trs):
                    def f(e):
                        e.bn_stats(stt[:, 0, :], src[:, 0:512])
                        return e.bn_stats(stt[:, 1, :], src[:, 512:1024])
                    S.op("dve", f, reads=[Tsrc], writes=[Tst])
                    S.op("dve", lambda e: e.bn_aggr(mv[:], stt[:].rearrange("p a b -> p (a b)")), reads=[Tst], writes=[Tmv])
                    S.op("dve", lambda e: e.tensor_scalar(rs[:, 0:1], mv[:, 1:2], EPS, None, ALU.add), reads=[Tmv], writes=[Trs])
                    S.op("pool", lambda e: e.tensor_tensor(rs[:, 0:1], rs[:, 0:1], mhalf[:, 0:1], ALU.pow), reads=[Trs, Tmh], writes=[Trs])
                    S.op("dve", lambda e: e.scalar_tensor_tensor(rs[:, 1:2], mv[:, 0:1], -1.0, rs[:, 0:1], ALU.mult, ALU.mult),
                         reads=[Tmv, Trs], writes=[Trs])

                def rotary(eng, src, Tsrc, rtab, Trtab, dst_f, Tdstf, tmp, Ttmp, out_ap=None, Tout=None):
                    s4 = src[:].rearrange("p (h a d) -> p h a d", h=4, a=2)
                    d4 = dst_f[:].rearrange("p (h a d) -> p h a d", h=4, a=2)
                    m4 = tmp[:].rearrange("p (h a d) -> p h a d", h=4, a=2)
                    cosb = rtab[:, 0:64].unsqueeze(1).unsqueeze(1).to_broadcast([128, 4, 2, 64])
                    nsin = rtab[:, 64:128].unsqueeze(1).to_broadcast([128, 4, 64])
                    psin = rtab[:, 128:192].unsqueeze(1).to_broadcast([128, 4, 64])
                    S.op(eng, lambda e: e.tensor_tensor(d4, s4, cosb, ALU.mult), reads=[Tsrc, Trtab], writes=[Tdstf])

                    def f(e):
                        e.tensor_tensor(m4[:, :, 0, :], s4[:, :, 1, :], nsin, ALU.mult)
                        return e.tensor_tensor(m4[:, :, 1, :], s4[:, :, 0, :], psin, ALU.mult)
                    S.op(eng, f, reads=[Tsrc, Trtab], writes=[Ttmp])
                    if out_ap is None:
                        S.op(eng, lambda e: e.tensor_tensor(dst_f[:], dst_f[:], tmp[:], ALU.add), reads=[Tdstf, Ttmp], writes=[Tdstf])
                    else:
                        S.op(eng, lambda e: e.tensor_tensor(out_ap, dst_f[:], tmp[:], ALU.add), reads=[Tdstf, Ttmp], writes=[Tout])

                gctr = [0]

                def nextbank():
                    bk = banks[gctr[0] % 2]
                    gctr[0] += 1
                    return bk

                def is_pre(b):
                    return b < npre

                def blk_src(b):
                    if is_pre(b):
                        return xm[b], b
                    return xm[b - npre], b - npre

                def load_block(b):
                    src_ap, ridx = blk_src(b)
                    S.dma("sp", xt[b % 4][:], src_ap, writes=[Txt[b % 4]])
                    S.dma("sp", rt[b % 3][:], rot[ridx], writes=[Trt[b % 3]])

                def s1_stats(b):
                    x_, Tx_ = xt[b % 4], Txt[b % 4]
                    ln_stats(x_, Tx_, stA, TstA, mvA, TmvA, rsA, TrsA)

                def s1_xn(b):
                    x_, Tx_ = xt[b % 4], Txt[b % 4]
                    S.op("act", lambda e: e.activation(xn[b % 2][:], x_[:], AF.Identity, bias=rsA[:, 1:2], scale=rsA[:, 0:1]),
                         reads=[Tx_, TrsA], writes=[Txn[b % 2]])

                def s1_tr(b):
                    def tr(e):
                        for k in range(8):
                            i = e.transpose(pT[:, k, :], xn[b % 2][:, k * 128:(k + 1) * 128], identb[:])
                        return i
                    S.op("pe", tr, reads=[Txn[b % 2], Tid], writes=[TpT])
                    u_ = uT[b % 2]

                    def ev(e):
                        for k in range(8):
                            i = e.activation(u_[:, k, :], pT[:, k, :], AF.Identity, bias=modc[:, 0, k:k + 1], scale=modc[:, 1, k:k + 1])
                        return i
                    S.op("act", ev, reads=[TpT, Tmodc], writes=[TuT[b % 2]])

                def proj_tok(b, c0, ncols, evac):
                    def piece():
                        bank, Tbank = nextbank()

                        def f(e):
                            for k in range(8):
                                i = e.matmul(bank[:, 0:ncols], lhsT=uT[b % 2][:, k, :], rhs=w_in_bf[:, k, c0:c0 + ncols],
                                             start=(k == 0), stop=(k == 7))
                            return i
                        S.op("pe", f, reads=[TuT[b % 2]] + TwinA + ([] if is_pre(b) else TwinB), writes=[Tbank])
                        evac(bank, Tbank)
                    return piece

                def proj_gk(b):
                    def piece():
                        bank, Tb = nextbank()

                        def fgk(e):
                            for k in range(8):
                                e.matmul(bank[:, 0:256], lhsT=uT[b % 2][:, k, :], rhs=w_in_bf[:, k, 2304:2560], start=(k == 0), stop=(k == 7))
                            for k in range(8):
                                ii = e.matmul(bank[0:16, 256:384], lhsT=w_in_bf[:, k, 3584:3600], rhs=uT[b % 2][:, k, :],
                                              start=(k == 0), stop=(k == 7))
                            return ii
                        S.op("pe", fgk, reads=[TuT[b % 2]] + TwinA + ([] if is_pre(b) else TwinB), writes=[Tb])
                        S.op("act", lambda e: e.activation(gkt[b % 2][:], bank[:, 0:256], AF.Copy), reads=[Tb], writes=[Tgkt[b % 2]])
                        S.op("act", lambda e: e.activation(glrT[b % 2][0:16, :], bank[0:16, 256:384], AF.Copy),
                             reads=[Tb], writes=[TglrT[b % 2]])
                    return piece

                def proj_gqk(b):
                    def piece():
                        bank, Tb = nextbank()

                        def fq(e):
                            for m in range(4):
                                for k in range(8):
                                    ii = e.matmul(bank[:, m * 128:(m + 1) * 128], lhsT=w_in_bf[:, k, 2048 + m * 128: 2048 + (m + 1) * 128],
                                                  rhs=uT[b % 2][:, k, :], start=(k == 0), stop=(k == 7))
                            return ii
                        S.op("pe", fq, reads=[TuT[b % 2]] + TwinA + ([] if is_pre(b) else TwinB), writes=[Tb])
                        S.op("act", lambda e: e.activation(gqkT[b % 2][:].rearrange("p a t -> p (a t)"), bank, AF.Copy),
                             reads=[Tb], writes=[TgqkT[b % 2]])
                    return piece

                def ev_copy(dst, Tdst):
                    return lambda bank, Tb: S.op("act", lambda e: e.activation(dst[:], bank, AF.Copy), reads=[Tb], writes=[Tdst])

                def ev_silu(dst, Tdst):
                    return lambda bank, Tb: S.op("act", lambda e: e.activation(dst[:], bank, AF.Silu), reads=[Tb], writes=[Tdst])

                def s2_pieces(b):
                    p = b % 2
                    if is_pre(b):
                        return [proj_tok(b, 512, 512, ev_copy(kf[p], Tkf[p])),
                                proj_tok(b, 1024, 512, ev_copy(rvb[p], Trvb[p])),
                                proj_gk(b),
                                proj_tok(b, 2560, 512, ev_copy(gvb[p], Tgvb[p]))]
                    rg_p = proj_tok(b, 1536, 512, ev_silu(srg[p], Tsrg[p]))
                    gg_p = proj_tok(b, 3072, 512, ev_silu(sgg[p], Tsgg[p]))

                    def silus():
                        rg_p()
                        gg_p()
                    return [proj_tok(b, 0, 512, ev_copy(qf[p], Tqf[p])),
                            proj_tok(b, 512, 512, ev_copy(kf[p], Tkf[p])),
                            proj_gk(b),
                            proj_tok(b, 1024, 512, ev_copy(rvb[p], Trvb[p])),
                            proj_gqk(b),
                            proj_tok(b, 2560, 512, ev_copy(gvb[p], Tgvb[p])),
                            silus]

                def g_logit(b):
                    p = b % 2
                    S.op("pe", lambda e: e.matmul(pG[:, 0:256], lhsT=glrT[p][:, :], rhs=wgs[:, :], start=True, stop=True),
                         reads=[TglrT[p], Twg], writes=[TpG])
                    S.op("act", lambda e: e.activation(spb[:], pG[:, 0:256], AF.Exp, scale=-1.0), reads=[TpG], writes=[Tspb])
                    S.op("act", lambda e: e.activation(spb[:], spb[:], AF.Ln, bias=1.0), reads=[Tspb], writes=[Tspb])

                def g_cums(b, pre):
                    p = b % 2
                    if pre:
                        def f(e):
                            e.matmul(pG[:, 256:512], lhsT=TriR, rhs=spb[:], start=True, stop=True)
                            e.matmul(pP[:, 0:1], lhsT=spb[:, 0:128], rhs=n16, start=True, stop=True)
                            return e.matmul(pP[:, 1:2], lhsT=spb[:, 128:256], rhs=n16, start=True, stop=True)
                        S.op("pe", f, reads=[Tspb, Tcs], writes=[TpG, TpP])
                        S.op("act", lambda e: e.activation(Erev[:], pG[:, 256:512], AF.Exp), reads=[TpG], writes=[TErev])
                        S.op("act", lambda e: e.activation(eBl[:], pP[:, 0:2], AF.Exp), reads=[TpP], writes=[TeBl])
                    else:
                        pBT = pP[:, 0:256].rearrange("p (a t) -> p a t", a=2)

                        def f(e):
                            e.matmul(pG[:, 256:512], lhsT=TriR, rhs=spb[:], start=True, stop=True)
                            e.matmul(pBT[:, 0, :], lhsT=spb[:, 0:128], rhs=TriN, start=True, stop=True)
                            return e.matmul(pBT[:, 1, :], lhsT=spb[:, 128:256], rhs=TriN, start=True, stop=True)
                        S.op("pe", f, reads=[Tspb, Tcs], writes=[TpG, TpP])
                        S.op("act", lambda e: e.activation(Erev[:], pG[:, 256:512], AF.Exp), reads=[TpG], writes=[TErev])
                        S.op("act", lambda e: e.activation(EpT[:], pBT, AF.Exp), reads=[TpP], writes=[TEpT])
                        S.op("act", lambda e: e.activation(EmT[:], pBT, AF.Exp, scale=-1.0), reads=[TpP], writes=[TEmT])
                        S.op("act", lambda e: e.activation(eBl[:], EpT[:, :, 127], AF.Copy), reads=[TEpT], writes=[TeBl])
                    S.op("pool", lambda e: e.tensor_tensor(Kd[:], gkt[p][:], Erev[:], ALU.mult), reads=[Tgkt[p], TErev], writes=[TKd])

                def k_rot(b, want_kr):
                    p = b % 2
                    rotary("dve", kf[p], Tkf[p], rt[b % 3], Trt[b % 3], t1, Tt1, t2, Tt2)
                    if want_kr:
                        S.op("dve", lambda e: e.tensor_copy(kr[:], t1[:]), reads=[Tt1], writes=[Tkr])
                    S.op("pool", lambda e: e.tensor_tensor(
                        kd[:].rearrange("p (h d) -> p h d", h=4), t1[:].rearrange("p (h d) -> p h d", h=4),
                        kdec.unsqueeze(2).to_broadcast([128, 4, 128]), ALU.mult), reads=[Tt1, Tcs], writes=[Tkd])

                def st_U(b):
                    p = b % 2

                    def fu(e):
                        for h in range(4):
                            i = e.matmul(pU[:, h * 128:(h + 1) * 128], lhsT=kd[:, h * 128:(h + 1) * 128],
                                         rhs=rvb[p][:, h * 128:(h + 1) * 128], start=True, stop=True)
                        return i
                    S.op("pe", fu, reads=[Tkd, Trvb[p]], writes=[TpU])

                    def fr(e):
                        for h in range(4):
                            i = e.scalar_tensor_tensor(R32[:, h, :], R32[:, h, :], float(GAM[h] ** 128),
                                                       pU[:, h * 128:(h + 1) * 128], ALU.mult, ALU.add)
                        return i
                    S.op("dve", fr, reads=[TpU, TR32], writes=[TR32])

                def st_Un(b):
                    p = b % 2
                    pV, TpV = (pS, TpS) if is_pre(b) else (pP, TpP)

                    def fg(e):
                        for pr in range(2):
                            i = e.matmul(pV[:, pr * 256:(pr + 1) * 256], lhsT=Kd[:, pr * 128:(pr + 1) * 128],
                                         rhs=gvb[p][:, pr * 256:(pr + 1) * 256], start=True, stop=True)
                        return i
                    S.op("pe", fg, reads=[TKd, Tgvb[p]], writes=[TpV])

                    def fs(e):
                        for pr in range(2):
                            for hh in range(2):
                                rsl = slice(hh * 64, (hh + 1) * 64)
                                i = e.scalar_tensor_tensor(
                                    S32[rsl, pr, :], S32[rsl, pr, :], eBl[rsl, pr:pr + 1],
                                    pV[rsl, pr * 256 + hh * 128: pr * 256 + (hh + 1) * 128], ALU.mult, ALU.add)
                        return i
                    S.op("dve", fs, reads=[TpV, TS32, TeBl], writes=[TS32])
                    if is_pre(b):
                        S.op("dve", lambda e: e.tensor_tensor(Dacc[:], Dacc[:], eBl[:], ALU.mult), reads=[TeBl, TDacc], writes=[TDacc])

                def s3_pre(b):
                    return {"R": [lambda: k_rot(b, False), lambda: st_U(b)],
                            "G": [lambda: g_logit(b), lambda: g_cums(b, True), lambda: st_Un(b)],
                            "B": [], "tailS": []}

                def s3_main(b):
                    p = b % 2
                    n = b - npre
                    o3 = pO[:].rearrange("p (h e) -> p h e", h=4)
                    g3 = pG[:].rearrange("p (h e) -> p h e", h=4)
                    t33 = t3[:].rearrange("p (h e) -> p h e", h=4)
                    t43 = t4[:].rearrange("p (h e) -> p h e", h=4)
                    mx = mixed[p]
                    Tmx = Tmixed[p]

                    def rotq():
                        rotary("pool", qf[p], Tqf[p], rt[b % 3], Trt[b % 3], t3, Tt3, t4, Tt4, out_ap=qr[:], Tout=Tqr)

                    def rotk():
                        rotary("dve", kf[p], Tkf[p], rt[b % 3], Trt[b % 3], t1, Tt1, t2, Tt2)
                        S.op("dve", lambda e: e.tensor_copy(kr[:], t1[:]), reads=[Tt1], writes=[Tkr])
                        S.op("pool", lambda e: e.tensor_tensor(
                            kd[:].rearrange("p (h d) -> p h d", h=4), t1[:].rearrange("p (h d) -> p h d", h=4),
                            kdec.unsqueeze(2).to_broadcast([128, 4, 128]), ALU.mult), reads=[Tt1, Tcs], writes=[Tkd])

                    def P2():
                        def trqk(e):
                            for h in range(4):
                                e.transpose(pT[:, h, :], qr[:, h * 128:(h + 1) * 128], identb[:])
                            for h in range(4):
                                ii = e.transpose(pT[:, 4 + h, :], kr[:, h * 128:(h + 1) * 128], identb[:])
                            return ii
                        S.op("pe", trqk, reads=[Tqr, Tkr, Tid], writes=[TpT])
                        S.op("act", lambda e: e.activation(qkT[:], pT[:], AF.Copy), reads=[TpT], writes=[TqkT])

                    def P3a():
                        g_cums(b, False)

                    def P3b():
                        Qp4 = QpT[:].rearrange("p (a b) t -> p a b t", b=2)
                        Qm4 = QmT[:].rearrange("p (a b) t -> p a b t", b=2)
                        gq = gqkT[p]

                        def fqp(e):
                            for hh in range(2):
                                rsl = slice(hh * 64, (hh + 1) * 64)
                                ii = e.scalar_tensor_tensor(Qp4[rsl, :, hh, :], gq[rsl, 0:2, :], 0.125, EpT[rsl, :, :], ALU.mult, ALU.mult)
                            return ii
                        S.op("dve", fqp, reads=[TgqkT[p], TEpT], writes=[TQpT])
                        S.op("pool", lambda e: e.tensor_tensor(KmT[:], gq[:, 2:4, :], EmT[:], ALU.mult), reads=[TgqkT[p], TEmT], writes=[TKmT])

                        def fqm(e):
                            for hh in range(2):
                                rsl = slice(hh * 64, (hh + 1) * 64)
                                ii = e.scalar_tensor_tensor(Qm4[rsl, :, hh, :], gq[rsl, 0:2, :], 0.125, EmT[rsl, :, :], ALU.mult, ALU.mult)
                            return ii
                        S.op("dve", fqm, reads=[TgqkT[p], TEmT], writes=[TQmT])
                        S.op("pool", lambda e: e.tensor_tensor(KpT[:], gq[:, 2:4, :], EpT[:], ALU.mult), reads=[TgqkT[p], TEpT], writes=[TKpT])

                    def P4():
                        def fsc(e):
                            for h in range(4):
                                ii = e.matmul(pS[:, h * 128:(h + 1) * 128], lhsT=qkT[:, 4 + h, :], rhs=qkT[:, h, :], start=True, stop=True)
                            return ii
                        S.op("pe", fsc, reads=[TqkT], writes=[TpS])
                        S.op("dve", lambda e: e.tensor_tensor(scT[:], pS[:].rearrange("p (h t) -> p h t", h=4), maskR, ALU.mult),
                             reads=[TpS, Tcs], writes=[TscT])

                    def P6a():
                        def fo(e):
                            for h in range(4):
                                e.matmul(pO[:, h * 128:(h + 1) * 128], lhsT=scT[:, h, :], rhs=rvb[p][:, h * 128:(h + 1) * 128], start=True, stop=False)
                                ii = e.matmul(pO[:, h * 128:(h + 1) * 128], lhsT=qkT[:, h, :], rhs=Rbf[:, h, :], start=False, stop=True)
                            return ii
                        S.op("pe", fo, reads=[TscT, Trvb[p], TqkT, TRbf], writes=[TpO])

                        def fbs(e):
                            for h in range(4):
                                ii = e.bn_stats(st4[:, h, :], pO[:, h * 128:(h + 1) * 128])
                            return ii
                        S.op("dve", fbs, reads=[TpO], writes=[Tst4])

                        def fba(e):
                            for h in range(4):
                                ii = e.bn_aggr(mv4[:, h, :], st4[:, h, :])
                            return ii
                        S.op("dve", fba, reads=[Tst4], writes=[Tmv4])
                        S.op("dve", lambda e: e.tensor_tensor(rs4[:], mv4[:, :, 1], epsr, ALU.add), reads=[Tmv4, Tcs], writes=[Trs4])

                    def P6b():
                        S.op("pool", lambda e: e.tensor_tensor(rs4[:], rs4[:], mhalf[:], ALU.pow), reads=[Trs4, Tmh], writes=[Trs4])
                        S.op("dve", lambda e: e.tensor_tensor(t33, o3, mv4[:, :, 0:1].to_broadcast([128, 4, 128]), ALU.subtract),
                             reads=[TpO, Tmv4], writes=[Tt3])

                    def P6c():
                        S.op("pool", lambda e: e.tensor_tensor(t33, t33, rs4[:].unsqueeze(2).to_broadcast([128, 4, 128]), ALU.mult),
                             reads=[Tt3, Trs4], writes=[Tt3])
                        S.op("pool", lambda e: e.tensor_tensor(t3[:], t3[:], rowbc[:, 2, 0:512], ALU.mult), reads=[Tt3, Trow], writes=[Tt3])

                    def P6d():
                        S.op("dve", lambda e: e.tensor_tensor(mx[:, 0:512], t3[:], srg[p][:], ALU.mult), reads=[Tt3, Tsrg[p]], writes=[Tmx])

                    def P5a():
                        def fp1(e):
                            for h in range(4):
                                ii = e.matmul(pS[:, h * 128:(h + 1) * 128], lhsT=KmT[:, h // 2, :], rhs=QpT[:, h, :], start=True, stop=True)
                            return ii
                        S.op("pe", fp1, reads=[TKmT, TQpT], writes=[TpS])

                        def fp2(e):
                            for h in range(4):
                                ii = e.matmul(pP[:, h * 128:(h + 1) * 128], lhsT=KpT[:, h // 2, :], rhs=QmT[:, h, :], start=True, stop=True)
                            return ii
                        S.op("pe", fp2, reads=[TKpT, TQmT], writes=[TpP])
                        t13 = t1[:].rearrange("p (h t) -> p h t", h=4)
                        t23 = t2[:].rearrange("p (h t) -> p h t", h=4)
                        S.op("dve", lambda e: e.tensor_tensor(t13, pS[:].rearrange("p (h t) -> p h t", h=4),
                                                              ML.unsqueeze(1).to_broadcast([128, 4, 128]), ALU.mult),
                             reads=[TpS, Tcs], writes=[Tt1])
                        S.op("dve", lambda e: e.tensor_tensor(t23, pP[:].rearrange("p (h t) -> p h t", h=4),
                                                              MU.unsqueeze(1).to_broadcast([128, 4, 128]), ALU.mult),
                             reads=[TpP, Tcs], writes=[Tt2])

                    def P5b():
                        S.op("pool", lambda e: e.tensor_tensor(AT[:].rearrange("p h t -> p (h t)"), t1[:], t2[:], ALU.add),
                             reads=[Tt1, Tt2], writes=[TAT])

                    def P7a():
                        def fog(e):
                            for h in range(4):
                                e.matmul(pG[:, h * 128:(h + 1) * 128], lhsT=AT[:, h, :], rhs=gvb[p][:, h * 128:(h + 1) * 128], start=True, stop=False)
                                ii = e.matmul(pG[:, h * 128:(h + 1) * 128], lhsT=QpT[:, h, :], rhs=Sbf[:, h // 2, :], start=False, stop=True)
                            return ii
                        S.op("pe", fog, reads=[TAT, Tgvb[p], TQpT, TSbf], writes=[TpG])
                        S.op("act", lambda e: e.activation(t4[:], pG[:], AF.Square), reads=[TpG], writes=[Tt4])
                        S.op("dve", lambda e: e.tensor_reduce(ss4[:], t43, AX.X, ALU.add), reads=[Tt4], writes=[Tss4])
                        S.op("dve", lambda e: e.tensor_scalar(rg4[:], ss4[:], 1.0 / 128.0, EPS, ALU.mult, ALU.add), reads=[Tss4], writes=[Trg4])

                    def P7b():
                        S.op("pool", lambda e: e.tensor_tensor(rg4[:], rg4[:], mhalf[:], ALU.pow), reads=[Trg4, Tmh], writes=[Trg4])
                        S.op("dve", lambda e: e.tensor_tensor(t43, g3, rg4[:].unsqueeze(2).to_broadcast([128, 4, 128]), ALU.mult),
                             reads=[TpG, Trg4], writes=[Tt4])

                    def P7c():
                        S.op("pool", lambda e: e.tensor_tensor(t4[:], t4[:], rowbc[:, 2, 512:1024], ALU.mult), reads=[Tt4, Trow], writes=[Tt4])
                        S.op("pool", lambda e: e.tensor_tensor(mx[:, 512:1024], t4[:], sgg[p][:], ALU.mult), reads=[Tt4, Tsgg[p]], writes=[Tmx])

                    def state():
                        st_U(b)
                        st_Un(b)
                        S.op("act", lambda e: e.activation(Rbf[:], R32[:], AF.Copy), reads=[TR32], writes=[TRbf])
                        S.op("act", lambda e: e.activation(Sbf[:], S32[:], AF.Copy), reads=[TS32], writes=[TSbf])

                    return {"R": [rotq, rotk, P2, P4, P6a, P6b, P6c, P6d],
                            "G": [lambda: g_logit(b), P3a, P3b, P5a, P5b, P7a, P7b, P7c],
                            "B": [], "tailS": [state]}

                def s3b_main(b):
                    p = b % 2
                    n = b - npre
                    mx, Tmx = mixed[p], Tmixed[p]
                    x_, Tx_ = xt[b % 4], Txt[b % 4]
                    xo, Txo = x1o[n % 2], Tx1o[n % 2]

                    def P8():
                        def trm(e):
                            for k in range(8):
                                ii = e.transpose(pT[:, k, :], mx[:, k * 128:(k + 1) * 128], identb[:])
                            return ii
                        S.op("pe", trm, reads=[Tmx, Tid], writes=[TpT])
                        S.op("act", lambda e: e.activation(mixT[:, 0:4, :], pT[:, 0:4, :], AF.Copy), reads=[TpT], writes=[TmixT])
                        S.op("act", lambda e: e.activation(mixT[:, 4:8, :], pT[:, 4:8, :], AF.Copy), reads=[TpT], writes=[TmixTb])

                    def P9a():
                        for c in range(2):
                            bank, Tb = nextbank()

                            def fm(e, k0, k1, bank=bank, c=c):
                                for k in range(k0, k1):
                                    ii = e.matmul(bank, lhsT=mixT[:, k, :], rhs=w_out_bf[:, k, c * 512:(c + 1) * 512],
                                                  start=(k == 0), stop=(k == 7))
                                return ii
                            S.op("pe", lambda e, fm=fm: fm(e, 0, 4), reads=[TmixT, Twout], writes=[Tb])
                            S.op("pe", lambda e, fm=fm: fm(e, 4, 8), reads=[TmixTb, Twout], writes=[Tb])
                            S.op("dve", lambda e, bank=bank, c=c: e.tensor_tensor(
                                xo[:, c * 512:(c + 1) * 512], bank, gate_bc[:, 0, c * 512:(c + 1) * 512], ALU.mult),
                                reads=[Tb, Tgate], writes=[Txo])
                        S.op("dve", lambda e: e.scalar_tensor_tensor(xo[:], x_[:], ALPHA, xo[:], ALU.mult, ALU.add),
                             reads=[Tx_, Txo], writes=[Txo])

                    def P9b():
                        ln_stats(xo, Txo, stB, TstB, mvB, TmvB, rsB, TrsB)

                    def P9c():
                        S.op("act", lambda e: e.activation(xo[:], xo[:], AF.Identity, bias=rsB[:, 1:2], scale=rsB[:, 0:1]),
                             reads=[Txo, TrsB], writes=[Txo])
                        S.op("dve", lambda e: e.tensor_tensor(xo[:], xo[:], rowbc[:, 0, :], ALU.mult), reads=[Txo, Trow], writes=[Txo])

                    def P9d():
                        S.op("pool", lambda e: e.tensor_tensor(xo[:], xo[:], rowbc[:, 1, :], ALU.add), reads=[Txo, Trow], writes=[Txo])
                        S.dma("sp", x1d[n], xo[:], reads=[Txo], writes=[Tx1d[n]], key="d:x1o%d" % (n % 2))
                    return [P8, P9a, P9b, P9c, P9d]

                nblk = npre + nb
                for b0 in range(min(3, nblk)):
                    load_block(b0)
                pending = list(range(8, 24))
                if npre < 12:
                    for i_ in pending:
                        ada_sub(i_)
                    pending = []
                for t in range(-2, nblk + 1):
                    chains = []
                    tails = []
                    if 0 <= t < nblk:
                        d = s3_pre(t) if is_pre(t) else s3_main(t)
                        if t == npre and npre > 0:
                            exchange()
                            d["R"].insert(4, combine)
                        chains += [d["R"], d["G"]]
                        tails += d["tailS"]
                    if 0 <= t - 1 < nblk and not is_pre(t - 1):
                        chains.append(s3b_main(t - 1))
                    if 0 <= t + 1 < nblk:
                        chains.append(s2_pieces(t + 1))
                    seq = []
                    for i in range(max([len(c_) for c_ in chains] + [0])):
                        for c_ in chains:
                            if i < len(c_):
                                seq.append(c_[i])
                    if 0 <= t + 2 < nblk:
                        st_p = lambda bb=t + 2: s1_stats(bb)
                        xn_p = lambda bb=t + 2: s1_xn(bb)
                        tr_p = lambda bb=t + 2: s1_tr(bb)
                        if 0 <= t < nblk and is_pre(t):
                            k_rot_, logit_, rk_, stU_, cums_, rv_, stUn_ = seq[0:7]
                            seq = [k_rot_, st_p, logit_, rk_, cums_, xn_p, rv_, stU_, stUn_, tr_p] + seq[7:]
                        elif 0 <= t < nblk:
                            seq.insert(min(5, len(seq)), st_p)
                            seq.insert(min(11, len(seq)), xn_p)
                            seq.append(tr_p)
                        else:
                            seq = [st_p, xn_p] + seq + [tr_p]
                    seq += tails
                    for fn_ in seq:
                        fn_()
                    if 3 <= t + 3 < nblk:
                        load_block(t + 3)
                    if pending and t >= 0:
                        ada_sub(pending.pop(0))
                    if t + 1 == nblk - 1 and nb > 0:
                        S.dma("pool", w1_bf[:], w_ff1.rearrange("(k p) n -> p k n", p=128), writes=[Tw1] + TwinA + TwinB)
                    if t == 0:
                        ck(4)
                    if t == npre:
                        ck(9)
                barrier = S.all_compute_evs() + [(k, S.cnt[k]) for k in ("d:x1o0", "d:x1o1") if k in S.cnt]
            with contextlib.ExitStack() as st2:
                def sb2(name, shape, dt=F32):
                    return st2.enter_context(nc.sbuf_tensor(name, shape, dt))
                rowbc2 = sb2("rowbc2", [128, 2, 1024]); Trow2 = T("rowbc2")
                w2_bf = sb2("w2_bf", [128, 32, 1024], BF16); Tw2 = T("w2_bf")
                for ri, r in enumerate((2, 3)):
                    S.dma("sp", rowbc2[:, ri, :], rows[r:r + 1, :].to_broadcast([128, 1024]), writes=[Trow2], extra=barrier)
                S.dma("pool", w2_bf[:], w_ff2.rearrange("(k p) n -> p k n", p=128), writes=[Tw2], extra=barrier)
                ck(10)
                NG = nb // GRP
                x1g = [sb2("x1g%d" % i, [128, GRP, 1024]) for i in range(2)]
                Tx1g = [[T("x1g%d_%d" % (i, j)) for j in range(GRP)] for i in range(2)]
                u2T = [sb2("u2T%d" % i, [128, 8, GRP * 128], BF16) for i in range(2)]; Tu2T = [T("u2T%d" % i) for i in range(2)]
                hT = sb2("hT", [128, 32, GRP * 128], BF16); ThT = [T("hT%d" % i) for i in range(16)]
                rl = [sb2("rl%d" % i, [128, 2 * GRP * 128]) for i in range(2)]; Trl = [T("rl%d" % i) for i in range(2)]
                xn2 = [sb2("xn2_%d" % i, [128, 1024], BF16) for i in range(GRP)]
                stC = sb2("stC", [128, 2, 6]); TstC = T("stC")
                mvC = sb2("mvC", [128, 2]); TmvC = T("mvC")
                rsC = sb2("rsC", [128, 2]); TrsC = T("rsC")
                stD = sb2("stD", [128, 2, 6]); TstD = T("stD")
                mvD = sb2("mvD", [128, 2]); TmvD = T("mvD")
                rsD = sb2("rsD", [128, 2]); TrsD = T("rsD")
                x2p = sb2("x2p", [128, 1024]); Tx2p = T("x2p")
                ob = [sb2("ob%d" % i, [128, 1024]) for i in range(2)]; Tob = [T("ob%d" % i) for i in range(2)]
                Txn2 = [T("xn2_%d" % i) for i in range(GRP)]
                first_done = set()

                def guard(Tt):
                    if Tt.name not in first_done:
                        first_done.add(Tt.name)
                        for (k, v) in barrier:
                            if Tt.r.get(k, 0) < v:
                                Tt.r[k] = v
                for tt in (ThT + [TstC, TmvC, TrsC, TstD, TmvD, TrsD, Tx2p] + Txn2 + Tu2T + Trl + Tob + Tx1g[0] + Tx1g[1]):
                    guard(tt)
                fb = [(pS, TpS), (pO, TpO), (pU, TpU), (pG, TpG)]

                stC2 = [sb2("stC2_%d" % i, [128, 2, 6]) for i in range(GRP)]; TstC2 = [T("stC2_%d" % i) for i in range(GRP)]
                mvC2 = [sb2("mvC2_%d" % i, [128, 2]) for i in range(GRP)]; TmvC2 = [T("mvC2_%d" % i) for i in range(GRP)]
                rsC2 = [sb2("rsC2_%d" % i, [128, 2]) for i in range(GRP)]; TrsC2 = [T("rsC2_%d" % i) for i in range(GRP)]
                for tt in TstC2 + TmvC2 + TrsC2:
                    guard(tt)

                def ln2_a1(g, j):
                    gp = g % 2
                    blk = g * GRP + j
                    src = x1g[gp][:, j, :]
                    Tsrc = Tx1g[gp][j]
                    S.dma("sp", src, x1d[blk], reads=[Tx1d[blk]], writes=[Tsrc])

                    def f(e):
                        e.bn_stats(stC2[j][:, 0, :], src[:, 0:512])
                        return e.bn_stats(stC2[j][:, 1, :], src[:, 512:1024])
                    S.op("dve", f, reads=[Tsrc], writes=[TstC2[j]])
                    S.op("dve", lambda e: e.bn_aggr(mvC2[j][:], stC2[j][:].rearrange("p a b -> p (a b)")), reads=[TstC2[j]], writes=[TmvC2[j]])

                def ln2_a2(g, j):
                    gp = g % 2
                    src = x1g[gp][:, j, :]
                    Tsrc = Tx1g[gp][j]
                    rs_, Trs_, mv_, Tmv_ = rsC2[j], TrsC2[j], mvC2[j], TmvC2[j]
                    S.op("act", lambda e: e.activation(rs_[:, 0:1], mv_[:, 1:2], AF.Ln, bias=EPS), reads=[Tmv_], writes=[Trs_])
                    S.op("act", lambda e: e.activation(rs_[:, 0:1], rs_[:, 0:1], AF.Exp, scale=-0.5), reads=[Trs_], writes=[Trs_])
                    S.op("dve", lambda e: e.scalar_tensor_tensor(rs_[:, 1:2], mv_[:, 0:1], -1.0, rs_[:, 0:1], ALU.mult, ALU.mult),
                         reads=[Tmv_, Trs_], writes=[Trs_])
                    S.op("act", lambda e: e.activation(xn2[j][:], src, AF.Identity, bias=rs_[:, 1:2], scale=rs_[:, 0:1]),
                         reads=[Tsrc, Trs_], writes=[Txn2[j]])

                def ln2_a(g, j):
                    ln2_a1(g, j)
                    ln2_a2(g, j)

                def ln2_b(g, j):
                    gp = g % 2

                    def tr(e):
                        for k in range(8):
                            i = e.transpose(pT[:, k, :], xn2[j][:, k * 128:(k + 1) * 128], identb[:])
                        return i
                    S.op("pe", tr, reads=[Txn2[j], Tid], writes=[TpT])

                    def ev(e):
                        for k in range(8):
                            i = e.activation(u2T[gp][:, k, j * 128:(j + 1) * 128], pT[:, k, :], AF.Identity,
                                             bias=modc[:, 2, k:k + 1], scale=modc[:, 3, k:k + 1])
                        return i
                    S.op("act", ev, reads=[TpT, Tmodc], writes=[Tu2T[gp]])

                def ln2uT(g):
                    for j in range(GRP):
                        ln2_a(g, j)
                        ln2_b(g, j)

                if NG > 0:
                    ln2uT(0)
                    ck(11)
                for g in range(NG):
                    gp = g % 2
                    hT2 = hT[:].rearrange("p (a b) t -> p a (b t)", b=2)
                    for jj in range(16):
                        fbank, Tfb = fb[jj % 4]

                        def f1(e, fbank=fbank, jj=jj):
                            for b_ in range(2):
                                j = 2 * jj + b_
                                for k in range(8):
                                    ii = e.matmul(fbank[:, b_ * 256:(b_ + 1) * 256], lhsT=w1_bf[:, k, j * 128:(j + 1) * 128],
                                                  rhs=u2T[gp][:, k, :], start=(k == 0), stop=(k == 7))
                            return ii
                        S.op("pe", f1, reads=[Tw1, Tu2T[gp]], writes=[Tfb])
                        rr, Trr = rl[jj % 2], Trl[jj % 2]
                        S.op("act", lambda e, fbank=fbank, rr=rr: e.activation(rr[:], fbank[:], AF.Relu), reads=[Tfb], writes=[Trr])
                        S.op("pool", lambda e, rr=rr, jj=jj: e.tensor_tensor(hT2[:, jj, :], rr[:], rr[:], ALU.mult), reads=[Trr], writes=[ThT[jj]])
                        if g + 1 < NG:
                            if jj == 0:
                                ln2_a1(g + 1, 0)
                                ln2_a1(g + 1, 1)
                            elif jj == 5:
                                ln2_a2(g + 1, 0)
                            elif jj == 7:
                                ln2_a2(g + 1, 1)
                            elif jj == 9:
                                ln2_b(g + 1, 0)
                            elif jj == 13:
                                ln2_b(g + 1, 1)
                    ck(12)
                    for j in range(GRP):
                        blk = g * GRP + j
                        for c in range(2):
                            bank, Tb = banks[c]

                            def f2(e, k0, k1, bank=bank, c=c, j=j):
                                for k in range(k0, k1):
                                    ii = e.matmul(bank, lhsT=hT[:, k, j * 128:(j + 1) * 128], rhs=w2_bf[:, k, c * 512:(c + 1) * 512],
                                                  start=(k == 0), stop=(k == 31))
                                return ii
                            S.op("pe", lambda e, f2=f2: f2(e, 0, 24), reads=ThT[0:12] + [Tw2], writes=[Tb])
                            S.op("pe", lambda e, f2=f2: f2(e, 24, 32), reads=ThT[12:16] + [Tw2], writes=[Tb])
                            S.op("dve", lambda e, bank=bank, c=c: e.tensor_tensor(
                                x2p[:, c * 512:(c + 1) * 512], bank, gate_bc[:, 1, c * 512:(c + 1) * 512], ALU.mult),
                                reads=[Tb, Tgate], writes=[Tx2p])
                        S.op("dve", lambda e, j=j: e.scalar_tensor_tensor(x2p[:], x1g[gp][:, j, :], ALPHA, x2p[:], ALU.mult, ALU.add),
                             reads=[Tx1g[gp][j], Tx2p], writes=[Tx2p])

                        def f(e):
                            e.bn_stats(stD[:, 0, :], x2p[:, 0:512])
                            return e.bn_stats(stD[:, 1, :], x2p[:, 512:1024])
                        S.op("dve", f, reads=[Tx2p], writes=[TstD])
                        S.op("dve", lambda e: e.bn_aggr(mvD[:], stD[:].rearrange("p a b -> p (a b)")), reads=[TstD], writes=[TmvD])
                        S.op("act", lambda e: e.activation(rsD[:, 0:1], mvD[:, 1:2], AF.Ln, bias=EPS), reads=[TmvD], writes=[TrsD])
                        S.op("act", lambda e: e.activation(rsD[:, 0:1], rsD[:, 0:1], AF.Exp, scale=-0.5), reads=[TrsD], writes=[TrsD])
                        S.op("dve", lambda e: e.scalar_tensor_tensor(rsD[:, 1:2], mvD[:, 0:1], -1.0, rsD[:, 0:1], ALU.mult, ALU.mult),
                             reads=[TmvD, TrsD], writes=[TrsD])
                        oo, Too = ob[blk % 2], Tob[blk % 2]
                        S.op("act", lambda e, oo=oo: e.activation(oo[:], x2p[:], AF.Identity, bias=rsD[:, 1:2], scale=rsD[:, 0:1]),
                             reads=[Tx2p, TrsD], writes=[Too])
                        S.op("pool", lambda e, oo=oo: e.tensor_tensor(oo[:], oo[:], rowbc2[:, 0, :], ALU.mult), reads=[Too, Trow2], writes=[Too])
                        S.op("pool", lambda e, oo=oo: e.tensor_tensor(oo[:], oo[:], rowbc2[:, 1, :], ALU.add), reads=[Too, Trow2], writes=[Too])
                        S.dma("sp", out[blk], oo[:], reads=[Too], key="d:ob%d" % (blk % 2))
                S.finish("sp", Tob + Tx1o + Tx1d)
                S._wait("sp", [(k, S.cnt[k]) for k in ("d:ob0", "d:ob1", "d:x1o0", "d:x1o1") if k in S.cnt])
        except StopBuild:
            pass
        S.dead = False
        S._wait("sp", S.all_compute_evs() + [(k, v) for k, v in S.cnt.items() if k.startswith("d:")])
        print("kernel build: instructions", S.nins, "waits", S.nwaits)
    return nc


def host_constants(q):
    cst = np.zeros((128, NCST), np.float32)
    cst[:, C_ID:C_ID + 128] = np.eye(128, dtype=np.float32)
    s = np.arange(128, dtype=np.float64)[:, None]
    t = np.arange(128, dtype=np.float64)[None, :]
    allowed = (np.floor(s / 64) <= np.floor(t / 64))
    for h in range(4):
        g = GAM[h]
        m = np.where(allowed, g ** np.abs(t - s) / g ** (t + 1.0), 0.0)
        cst[:, C_MR + h * 128:C_MR + (h + 1) * 128] = m.astype(np.float32)
        cst[:, C_KD + h] = (g ** (127.0 - s[:, 0])).astype(np.float32)
        fac2 = (128.0 ** -1.0) * g ** (2.0 * (s[:, 0] + 1.0))
        cst[:, C_ER + h] = (EPS / fac2).astype(np.float32)
    cst[:, C_ML:C_ML + 128] = (s <= t).astype(np.float32)
    cst[:, C_MU:C_MU + 128] = ((s > t) & (np.floor(s / 64) == np.floor(t / 64))).astype(np.float32)
    cst[:, C_TN:C_TN + 128] = np.where(s <= t, -1.0 / 16.0, 0.0).astype(np.float32)
    cst[:, C_TR:C_TR + 128] = np.where(s > t, -1.0 / 16.0, 0.0).astype(np.float32)
    cst[:, C_N16] = -1.0 / 16.0
    cst[:, C_ONE:C_ONE + 128] = 1.0
    for r in range(4):
        active = (r < q)
        cst[:, C_ACT + r] = 1.0 if active else 0.0
        for h in range(4):
            cst[:, C_AR + r * 4 + h] = np.float32(GAM[h] ** (NB * 128)) if active else 1.0
    return cst


def host_rot(q):
    pos = (q * NB * 128 + np.arange(NB * 128)).astype(np.float32)
    inv = (1.0 / (np.float32(10000.0) ** np.linspace(0.0, 1.0, 64, dtype=np.float32))).astype(np.float32)
    ang = (pos[:, None] * inv[None, :]).astype(np.float32)
    co = np.cos(ang).astype(np.float32)
    si = np.sin(ang).astype(np.float32)
    r = np.concatenate([co, -si, si], axis=1).astype(np.float32)
    return np.ascontiguousarray(r.reshape(NB, 128, 192))


_NC_CACHE = {}


def make_in_maps(inputs):
    x = np.asarray(inputs["x"], np.float32)
    c = np.asarray(inputs["c"], np.float32)
    rows = np.ascontiguousarray(np.stack([
        inputs["ln1_w"][0], inputs["ln1_b"][0], inputs["ln2_w"][0], inputs["ln2_b"][0],
        np.concatenate([inputs["ret_norm_w"][0], inputs["gla_norm_w"][0]])]).astype(np.float32))
    wg = np.ascontiguousarray(np.concatenate([inputs["gla_gate_w"][0], inputs["gla_gate_b"][0][None, :]], 0).astype(np.float32))
    shared = {
        "w_ada": np.ascontiguousarray(inputs["w_ada"][0], dtype=np.float32),
        "b_ada": np.ascontiguousarray(inputs["b_ada"][0][None, :], dtype=np.float32),
        "w_in": np.ascontiguousarray(inputs["w_in"][0], dtype=np.float32),
        "w_out": np.ascontiguousarray(inputs["w_out"][0], dtype=np.float32),
        "w_ff1": np.ascontiguousarray(inputs["w_ff1"][0], dtype=np.float32),
        "w_ff2": np.ascontiguousarray(inputs["w_ff2"][0], dtype=np.float32),
        "rows": rows, "wg": wg,
    }
    in_maps = []
    for j in range(8):
        b, q = j // 4, j % 4
        m = dict(shared)
        m["xm"] = np.ascontiguousarray(x[b, q * 2048:(q + 1) * 2048].reshape(NB, 128, 1024))
        m["c8"] = np.ascontiguousarray(c[b].reshape(8, 128).T)
        m["rot"] = host_rot(q)
        m["cst"] = host_constants(q)
        in_maps.append(m)
    return in_maps


def kernel(**inputs):
    if "nc" not in _NC_CACHE:
        _NC_CACHE["nc"] = build_nc()
    nc = _NC_CACHE["nc"]
    in_maps = make_in_maps(inputs)
    res = run_bass_kernel_spmd(nc, in_maps, core_ids=list(range(8)))
    outp = np.zeros((2, 8192, 1024), np.float32)
    for j in range(8):
        b, q = j // 4, j % 4
        outp[b, q * 2048:(q + 1) * 2048] = np.asarray(res.results[j]["out"]).reshape(2048, 1024)
    return outp
```
